# Optimizing a Trainium2 kernel written in Bass

```python
import jax, jax.numpy as jnp
from jax import lax
import numpy as np

D_MODEL = 1024
BATCH = 16
SEQ = 4096
DEPTH = 1

CHUNK = 64
D_MIX = D_MODEL
SB_WIDTH = D_MIX // 2
SB_HEADS = 8
SB_HEAD_DIM = SB_WIDTH // SB_HEADS
SB_QBLOCK = 128
ML_WIDTH = D_MIX - SB_WIDTH
ML_HEADS = 4
ML_HEAD_DIM = ML_WIDTH // ML_HEADS
CONV_K = 4
N_IN = 4 * SB_WIDTH + 5 * ML_WIDTH + 2 * ML_HEADS
EPS = 1e-6
GATE_SCALE = 0.1

kernel_name = "hymba_stickbreaking_mlstm_block"


def rms_norm(x, g):
    xf = x.astype(jnp.float32)
    y = xf * lax.rsqrt(jnp.mean(xf * xf, axis=-1, keepdims=True) + EPS)
    return (y * g.astype(jnp.float32)).astype(x.dtype)


def head_layer_norm(h, g):
    mu = jnp.mean(h, axis=-1, keepdims=True)
    hc = h - mu
    var = jnp.mean(hc * hc, axis=-1, keepdims=True)
    y = hc * lax.rsqrt(var + EPS)
    return y * g.astype(jnp.float32).reshape(h.shape[2], h.shape[3])


def causal_depthwise_conv(u, w, b):
    k = w.shape[0]
    y = lax.conv_general_dilated(
        u, w[:, None, :].astype(u.dtype), window_strides=(1,), padding=[(k - 1, 0)],
        dimension_numbers=("NWC", "WIO", "NWC"), feature_group_count=u.shape[-1])
    return y + b.astype(u.dtype)


def stick_breaking_attention(q, k, v):
    s_len, d = q.shape[2], q.shape[3]
    scale = d ** -0.5
    qf, kf, vf = q.astype(jnp.float32), k.astype(jnp.float32), v.astype(jnp.float32)
    outs = []
    for blk in range(s_len // SB_QBLOCK):
        q0 = blk * SB_QBLOCK
        kl = q0 + SB_QBLOCK
        qb = qf[:, :, q0:kl]
        kc = kf[:, :, :kl]
        vc = vf[:, :, :kl]
        z = jnp.einsum("bhqd,bhkd->bhqk", qb, kc) * scale
        qpos = q0 + jnp.arange(SB_QBLOCK)
        kpos = jnp.arange(kl)
        strict = kpos[None, :] < qpos[:, None]
        log_one_minus = jnp.where(strict, jax.nn.log_sigmoid(-z), 0.0)
        suffix = lax.cumsum(log_one_minus, axis=3, reverse=True) - log_one_minus
        a = jnp.where(strict, jnp.exp(jax.nn.log_sigmoid(z) + suffix), 0.0)
        outs.append(jnp.einsum("bhqk,bhkd->bhqd", a, vc))
    return jnp.concatenate(outs, axis=2)


def mlstm_chunkwise(q, k, v, i_pre, f_pre):
    bsz, s_len, nh, dk = q.shape
    dv = v.shape[-1]
    nc = s_len // CHUNK

    def to_chunks(t):
        return t.reshape(bsz, nc, CHUNK, nh, -1).transpose(0, 3, 1, 2, 4)

    def gate_chunks(t):
        return t.reshape(bsz, nc, CHUNK, nh).transpose(0, 3, 1, 2)

    qc = to_chunks(q)
    kc = to_chunks(k * (dk ** -0.5))
    vc = to_chunks(v)
    ig = gate_chunks(i_pre)
    logf = jax.nn.log_sigmoid(gate_chunks(f_pre))
    b = jnp.cumsum(logf, axis=-1)
    g = b[..., -1]
    logw = g[..., None] - b + ig

    def step(carry, xs):
        c_st, n_st, m_st = carry
        k_c, v_c, logw_c, g_c = xs
        m_new = jnp.maximum(g_c + m_st, jnp.max(logw_c, axis=-1))
        w = jnp.exp(logw_c - m_new[..., None])
        decay = jnp.exp(g_c + m_st - m_new)
        c_new = decay[..., None, None] * c_st + jnp.einsum("bhl,bhlk,bhlv->bhkv", w, k_c, v_c)
        n_new = decay[..., None] * n_st + jnp.einsum("bhl,bhlk->bhk", w, k_c)
        return (c_new, n_new, m_new), (c_st, n_st, m_st)

    init = (jnp.zeros((bsz, nh, dk, dv), jnp.float32),
            jnp.zeros((bsz, nh, dk), jnp.float32),
            jnp.zeros((bsz, nh), jnp.float32))
    xs = (kc.transpose(2, 0, 1, 3, 4), vc.transpose(2, 0, 1, 3, 4),
          logw.transpose(2, 0, 1, 3), g.transpose(2, 0, 1))
    _, (c_all, n_all, m_all) = lax.scan(step, init, xs)
    c_all = c_all.transpose(1, 2, 0, 3, 4)
    n_all = n_all.transpose(1, 2, 0, 3)
    m_all = m_all.transpose(1, 2, 0)

    causal = jnp.tril(jnp.ones((CHUNK, CHUNK), dtype=bool))
    dmat = b[..., :, None] - b[..., None, :] + ig[..., None, :]
    dmat = jnp.where(causal, dmat, -jnp.inf)
    m_inter = b + m_all[..., None]
    m_out = jnp.maximum(m_inter, jnp.max(dmat, axis=-1))
    scores = jnp.einsum("bhcld,bhcsd->bhcls", qc, kc) * jnp.exp(dmat - m_out[..., None])
    inter_w = jnp.exp(m_inter - m_out)
    num = (inter_w[..., None] * jnp.einsum("bhcld,bhcdv->bhclv", qc, c_all)
           + jnp.einsum("bhcls,bhcsv->bhclv", scores, vc))
    den = inter_w * jnp.einsum("bhcld,bhcd->bhcl", qc, n_all) + jnp.sum(scores, axis=-1)
    h = num / jnp.maximum(jnp.abs(den), jnp.exp(-m_out))[..., None]
    return h.transpose(0, 2, 3, 1, 4).reshape(bsz, s_len, nh, dv)


def setup_inputs(seed: int = 0) -> dict:
    key = jax.random.key(seed)
    ks = jax.random.split(key, 12)
    x = jax.random.normal(ks[0], (BATCH, SEQ, D_MODEL), jnp.float32)
    norm_g = 1.0 + 0.01 * jax.random.normal(ks[1], (DEPTH, D_MODEL), jnp.float32)
    col_scale = jnp.concatenate([jnp.ones((N_IN - 2 * ML_HEADS,), jnp.float32),
                                 jnp.full((2 * ML_HEADS,), GATE_SCALE, jnp.float32)])
    w_in = jax.random.normal(ks[2], (DEPTH, D_MODEL, N_IN), jnp.float32) * (D_MODEL ** -0.5) * col_scale
    b_igate = 0.1 * jax.random.normal(ks[3], (DEPTH, ML_HEADS), jnp.float32)
    b_fgate = (jnp.linspace(3.0, 6.0, ML_HEADS, dtype=jnp.float32)[None, :]
               + 0.01 * jax.random.normal(ks[4], (DEPTH, ML_HEADS), jnp.float32))
    conv_w = jax.random.normal(ks[5], (DEPTH, CONV_K, 2 * ML_WIDTH), jnp.float32) * (CONV_K ** -0.5)
    conv_b = 0.01 * jax.random.normal(ks[6], (DEPTH, 2 * ML_WIDTH), jnp.float32)
    head_norm_g = 1.0 + 0.01 * jax.random.normal(ks[7], (DEPTH, ML_WIDTH), jnp.float32)
    w_out = jax.random.normal(ks[8], (DEPTH, D_MIX, D_MODEL), jnp.float32) * (D_MIX ** -0.5)
    final_norm_g = 1.0 + 0.01 * jax.random.normal(ks[9], (D_MODEL,), jnp.float32)
    return {"x": x, "norm_g": norm_g, "w_in": w_in, "b_igate": b_igate, "b_fgate": b_fgate,
            "conv_w": conv_w, "conv_b": conv_b, "head_norm_g": head_norm_g,
            "w_out": w_out, "final_norm_g": final_norm_g}


def reference(x, norm_g, w_in, b_igate, b_fgate, conv_w, conv_b, head_norm_g, w_out, final_norm_g):
    bsz, s_len, _ = x.shape
    sizes = [SB_WIDTH] * 4 + [ML_WIDTH] * 5 + [ML_HEADS, ML_HEADS]
    split_at = [int(v) for v in np.cumsum(sizes)[:-1]]
    h = x
    for layer in range(DEPTH):
        u = rms_norm(h, norm_g[layer])
        p = u @ w_in[layer].astype(u.dtype)
        (sb_q, sb_k, sb_v, sb_z, m_q, m_k, m_v, m_o, m_z, m_i, m_f) = jnp.split(p, split_at, axis=-1)

        def sb_heads(t):
            return t.reshape(bsz, s_len, SB_HEADS, SB_HEAD_DIM).transpose(0, 2, 1, 3)
        sb = stick_breaking_attention(sb_heads(sb_q), sb_heads(sb_k), sb_heads(sb_v))
        sb = sb.transpose(0, 2, 1, 3).reshape(bsz, s_len, SB_WIDTH).astype(x.dtype)
        sb_out = sb * jax.nn.silu(sb_z)

        qk = causal_depthwise_conv(jnp.concatenate([m_q, m_k], axis=-1), conv_w[layer], conv_b[layer])
        mq, mk = jnp.split(qk, 2, axis=-1)

        def ml_heads(t):
            return t.reshape(bsz, s_len, ML_HEADS, ML_HEAD_DIM).astype(jnp.float32)
        i_pre = (m_i + b_igate[layer].astype(m_i.dtype)).astype(jnp.float32)
        f_pre = (m_f + b_fgate[layer].astype(m_f.dtype)).astype(jnp.float32)
        hm = mlstm_chunkwise(ml_heads(mq), ml_heads(mk), ml_heads(m_v), i_pre, f_pre)
        hm = hm * jax.nn.sigmoid(ml_heads(m_o))
        hm = head_layer_norm(hm, head_norm_g[layer])
        ml_out = hm.reshape(bsz, s_len, ML_WIDTH).astype(x.dtype) * jax.nn.silu(m_z)

        mix = jnp.concatenate([sb_out, ml_out], axis=-1)
        h = h + mix @ w_out[layer].astype(mix.dtype)
    return rms_norm(h, final_norm_g)
```

```python
import numpy as np
import concourse.bass as bass
import concourse.mybir as mybir
from concourse.bass_utils import run_bass_kernel_spmd
from contextlib import ExitStack

F32 = mybir.dt.float32
BF16 = mybir.dt.bfloat16
I32 = mybir.dt.int32
AF = mybir.ActivationFunctionType
ALU = mybir.AluOpType

D = 1024
NIN = 4616
TT = 256
EPS = 1e-6
C_SBQ, C_SBK, C_SBV, C_SBZ = 0, 512, 1024, 1536
C_MQ, C_MK, C_MV, C_MO, C_MZ, C_MI = 2048, 2560, 3072, 3584, 4096, 4608
WCH = [(0, 1024), (1024, 2048), (2048, 3072), (3072, 4096), (4096, NIN)]
ND_SEM = 12
DBG_STOP = 99


class Prog:
    def __init__(self):
        self.ops = []
        self.lastw = {}
        self.readers = {}
        self.ndma = 0
        self.slot_last = {}
        self.excl = set()

    def op(self, eng, fn, reads=(), writes=(), dma=False):
        idx = len(self.ops)
        deps = {}
        for r in reads:
            w = self.lastw.get(r)
            if w is not None:
                deps[w] = True
            if r in self.excl:
                for e2, ridx in self.readers.get(r, {}).items():
                    if e2 != eng and not isinstance(ridx, list):
                        deps.setdefault(ridx, False)
        for r in writes:
            w = self.lastw.get(r)
            if w is not None:
                deps.setdefault(w, False)
            for ridx in self.readers.get(r, {}).values():
                if isinstance(ridx, list):
                    for q in ridx:
                        deps.setdefault(q, False)
                else:
                    deps.setdefault(ridx, False)
        for r in reads:
            d = self.readers.setdefault(r, {})
            if dma:
                d.setdefault('dma', []).append(idx)
            else:
                d[eng] = idx
        for r in writes:
            self.lastw[r] = idx
            self.readers[r] = {}
        slot = None
        if dma:
            slot = self.ndma % ND_SEM
            self.ndma += 1
            prev = self.slot_last.get(slot)
            if prev is not None:
                deps.setdefault(prev, False)
            self.slot_last[slot] = idx
        self.ops.append(dict(eng=eng, fn=fn, deps=deps, dma=dma, signal=False, slot=slot))
        return idx

    def emit(self, nc, stack):
        ops = self.ops
        for o in ops:
            for d, raw in o['deps'].items():
                p = ops[d]
                if p['dma'] or o['dma']:
                    p['signal'] = True
                elif p['eng'] != o['eng'] or raw or o['eng'] != 'pe':
                    p['signal'] = True
        sems = {e: stack.enter_context(nc.semaphore("sem_" + e)) for e in ['pe', 'act', 'dve', 'pool']}
        dsems = [stack.enter_context(nc.semaphore("dsem%d" % i)) for i in range(ND_SEM)]
        cnt = {e: 0 for e in sems}
        dcnt = [0] * ND_SEM
        nd = 0
        for o in ops:
            if o['dma']:
                k = o['slot']
                dcnt[k] += 16
                o['sig'] = (k, dcnt[k])
                o['signal'] = True
            elif o['signal']:
                cnt[o['eng']] += 1
                o['sig'] = (o['eng'], cnt[o['eng']])
        per_eng = {e: [] for e in ['pe', 'act', 'dve', 'pool', 'sp']}
        for o in ops:
            per_eng[o['eng']].append(o)
        final_d = list(dcnt)

        def run(engname, e):
            seen = {}
            for o in per_eng[engname]:
                waits = {}
                for d, raw in o['deps'].items():
                    p = ops[d]
                    if not (p['dma'] or o['dma']):
                        if p['eng'] == o['eng'] and not raw and o['eng'] == 'pe':
                            continue
                    key, val = p['sig']
                    if val > waits.get(key, 0):
                        waits[key] = val
                for key, val in waits.items():
                    if seen.get(key, 0) >= val:
                        continue
                    seen[key] = val
                    s = dsems[key] if isinstance(key, int) else sems[key]
                    e.wait_ge(s, val)
                ins = o['fn'](e)
                if o['signal']:
                    if o['dma']:
                        ins.then_inc(dsems[o['sig'][0]], 16)
                    else:
                        ins.then_inc(sems[o['eng']], 1)
            if engname == 'sp':
                for k in range(ND_SEM):
                    if final_d[k] > 0:
                        e.wait_ge(dsems[k], final_d[k])

        with nc.Block() as block:
            @block.sync
            def _(e):
                run('sp', e)

            @block.tensor
            def _(e):
                run('pe', e)

            @block.vector
            def _(e):
                run('dve', e)

            @block.scalar
            def _(e):
                run('act', e)

            @block.gpsimd
            def _(e):
                run('pool', e)


def build(S, NB):
    nc = bass.Bass("TRN2", target_bir_lowering=False)
    NT = S // TT
    NKB = S // 128
    x = nc.dram_tensor("x", [NB, S, D], F32, kind="ExternalInput").ap()
    norm_g = nc.dram_tensor("norm_g", [1, D], F32, kind="ExternalInput").ap()
    w_in = nc.dram_tensor("w_in", [1, D, NIN], F32, kind="ExternalInput").ap()
    b_ig = nc.dram_tensor("b_igate", [1, 4], F32, kind="ExternalInput").ap()
    b_fg = nc.dram_tensor("b_fgate", [1, 4], F32, kind="ExternalInput").ap()
    conv_w = nc.dram_tensor("conv_w", [1, 4, 1024], F32, kind="ExternalInput").ap()
    conv_b = nc.dram_tensor("conv_b", [1, 1024], F32, kind="ExternalInput").ap()
    hng = nc.dram_tensor("head_norm_g", [1, 512], F32, kind="ExternalInput").ap()
    w_out = nc.dram_tensor("w_out", [1, 1024, 1024], F32, kind="ExternalInput").ap()
    fng = nc.dram_tensor("final_norm_g", [1024], F32, kind="ExternalInput").ap()
    out = nc.dram_tensor("out", [NB, S, D], F32, kind="ExternalOutput").ap()

    P = Prog()
    with ExitStack() as st:
        def SB(name, shape, dt):
            return st.enter_context(nc.sbuf_tensor(name, shape, dt))

        def PS(name, shape, dt=F32):
            return st.enter_context(nc.psum_tensor(name, shape, dt))

        win = SB("win", [128, 8, NIN], BF16)
        wout = SB("wout", [128, 8, 1024], BF16)
        kT_all = SB("kT_all", [128, 4, S], BF16)
        v_all = SB("v_all", [128, NKB, 512], BF16)
        ident = SB("ident", [128, 128], BF16)
        negtri = SB("negtri", [128, 128], BF16)
        mstrict = SB("mstrict", [128, 128], BF16)
        mincl = SB("mincl", [128, 128], BF16)
        onesb = SB("onesb", [128, 128], BF16)
        nlh = SB("nlh", [128, 8], BF16)
        LAll = SB("LAll", [128, 8, 32], BF16)
        Lcarry = SB("Lcarry", [32, 32], BF16)
        negsel = SB("negsel", [32, 8, 128], BF16)
        rhzero = SB("rhzero", [32, 256], BF16)
        iota_p = SB("iota_p", [128, 1], F32)
        iota_f = SB("iota_f", [128, 128], F32)
        iota_i = SB("iota_i", [128, 128], I32)
        g_sb = SB("g_sb", [128, 8], F32)
        cw_sb = SB("cw_sb", [128, 8, 4], F32)
        cb_sb = SB("cb_sb", [128, 8], F32)
        hg_sb = SB("hg_sb", [128, 4], F32)
        gbias = SB("gbias", [128, 8], F32)
        fg_bc = SB("fg_bc", [128, 1024], F32)
        one_c = SB("one_c", [128, 1], F32)
        eps_c = SB("eps_c", [128, 1], F32)
        xb = [SB("xb%d" % i, [128, 1024], F32) for i in range(2)]
        ub = SB("ub", [128, 1024], BF16)
        uT = SB("uT", [128, 8, TT], BF16)
        sm = SB("sm", [128, 64], F32)
        qT = [SB("qT%d" % i, [128, TT], BF16) for i in range(2)]
        szT = [SB("szT%d" % i, [128, TT], BF16) for i in range(2)]
        sgf = SB("sgf", [128, TT], F32)
        Et = [SB("E%d" % i, [128, 512], BF16) for i in range(2)]
        SPst = SB("SPst", [128, 4, 512], BF16)
        At = [SB("A%d" % i, [128, 512], BF16) for i in range(2)]
        RH = SB("RH", [32, 256], BF16)
        hist = SB("hist", [128, 8, 3], F32)
        pre = [SB("pre%d" % i, [128, TT + 3], F32) for i in range(2)]
        cacc = SB("cacc", [128, TT], F32)
        qTm = SB("qTm", [128, TT], BF16)
        kTm = SB("kTm", [128, TT], BF16)
        ktok = SB("ktok", [128, 2, 128], BF16)
        Vw = SB("Vw", [128, 129], BF16)
        Vu = SB("Vu", [128, 129], BF16)
        sgt = SB("sgt", [128, 256], F32)
        gz = SB("gz", [128, 128], F32)
        STm = SB("STm", [128, 128], BF16)
        Cst = SB("Cst", [128, 4, 129], F32)
        Cbf = SB("Cbf", [128, 4, 129], BF16)
        hmo = SB("hmo", [128, 128], F32)
        yb = SB("yb", [128, 128], F32)
        ml = SB("ml", [128, 128], BF16)
        gates = SB("gates", [128, 8], F32)
        gsm = SB("gsm", [128, 40], F32)
        bst = SB("bst", [128, 8], F32)
        mixT = SB("mixT", [128, 8, TT], BF16)

        zb = [PS("zb%d" % i, [128, 512]) for i in range(2)]
        rall = PS("rall", [128, 512])
        outTp = PS("outTp", [128, 512])
        pj = PS("pj", [128, 512])
        mA = PS("mA", [128, 512])
        mB = PS("mB", [128, 512])
        tp = PS("tp", [128, 1024], BF16)

        op = P.op
        P.excl = {'zb0', 'zb1', 'rall', 'outTp', 'pj', 'mA', 'mB', 'tp'}

        def wkeys(kc, c0, c1):
            return ["win.%d.%d" % (kc, i) for i, (a, b) in enumerate(WCH) if a < c1 and b > c0]

        def mm(out_, lhsT, rhs, start, stop, reads, writes):
            op('pe', lambda e: e.matmul(out_, lhsT=lhsT, rhs=rhs, start=start, stop=stop), reads, writes)

        def tr(out_, in_, reads, writes):
            op('pe', lambda e: e.transpose(out_, in_, ident[:]), list(reads) + ['ident'], writes)

        def act(out_, in_, func, reads, writes, bias=None, scale=None):
            kw = {}
            if bias is not None:
                kw['bias'] = bias
            if scale is not None:
                kw['scale'] = scale
            op('act', lambda e: e.activation(out=out_, in_=in_, func=func, **kw), reads, writes)

        def ts(eng, out_, in0, s1, s2, op0, op1, reads, writes):
            if op1 is None:
                op(eng, lambda e: e.tensor_scalar(out=out_, in0=in0, scalar1=s1, scalar2=None, op0=op0), reads, writes)
            else:
                op(eng, lambda e: e.tensor_scalar(out=out_, in0=in0, scalar1=s1, scalar2=s2, op0=op0, op1=op1), reads, writes)

        def tt(eng, out_, in0, in1, aop, reads, writes):
            op(eng, lambda e: e.tensor_tensor(out=out_, in0=in0, in1=in1, op=aop), reads, writes)

        def stt(out_, in0, scalar, in1, op0, op1, reads, writes, accum_out=None):
            if accum_out is None:
                op('dve', lambda e: e.scalar_tensor_tensor(out=out_, in0=in0, scalar=scalar, in1=in1, op0=op0, op1=op1),
                   reads, writes)
            else:
                op('dve', lambda e: e.scalar_tensor_tensor(out=out_, in0=in0, scalar=scalar, in1=in1, op0=op0, op1=op1,
                                                           accum_out=accum_out), reads, writes)

        def tcopy(eng, out_, in_, reads, writes):
            op(eng, lambda e: e.tensor_copy(out=out_, in_=in_), reads, writes)

        def recip(out_, in_, reads, writes):
            op('dve', lambda e: e.reciprocal(out=out_, in_=in_), reads, writes)

        def mset(eng, ap, val, writes):
            op(eng, lambda e: e.memset(ap, val), (), writes)

        def dma(out_, in_, reads, writes, slow=False):
            if slow:
                op('sp', lambda e: e.dma_start(out=out_, in_=in_, allow_slow_non_contiguous=True), reads, writes, dma=True)
            else:
                op('sp', lambda e: e.dma_start(out=out_, in_=in_), reads, writes, dma=True)

        def copy_op(eng, out_ap, in_ap, reads, writes, scale=None):
            if eng == 'act':
                act(out_ap, in_ap, AF.Copy, reads, writes, scale=scale)
            elif scale is None:
                tcopy(eng, out_ap, in_ap, reads, writes)
            else:
                ts(eng, out_ap, in_ap, scale, None, ALU.mult, None, reads, writes)

        op('pool', lambda e: e.iota(iota_i[:], pattern=[[1, 128]], base=0, channel_multiplier=0), (), ['iota_i'])
        tcopy('dve', iota_f[:], iota_i[:], ['iota_i'], ['iota_f'])
        op('pool', lambda e: e.iota(iota_i[:, 0:1], pattern=[[0, 1]], base=0, channel_multiplier=1), ['iota_f'], ['iota_i'])
        tcopy('dve', iota_p[:], iota_i[:, 0:1], ['iota_i'], ['iota_p'])
        cs = ['iota_f', 'iota_p']
        ts('dve', ident[:], iota_f[:], iota_p[:, 0:1], None, ALU.is_equal, None, cs, ['ident'])
        ts('dve', mstrict[:], iota_f[:], iota_p[:, 0:1], None, ALU.is_gt, None, cs, ['mstrict'])
        ts('dve', mincl[:], iota_f[:], iota_p[:, 0:1], None, ALU.is_ge, None, cs, ['mincl'])
        ts('dve', negtri[:], iota_f[:], iota_p[:, 0:1], -1.0, ALU.is_le, ALU.mult, cs, ['negtri'])
        mset('dve', rhzero[:], 0.0, ['rhzero'])
        for i in range(8):
            ts('dve', negsel[:, i, :], rhzero[:, 0:128], iota_p[0:32, 0:1], float(i), ALU.add, ALU.is_equal,
               ['rhzero', 'iota_p'], ['negsel'])
        ts('dve', negsel[:], negsel[:], -1.0, None, ALU.mult, None, ['negsel'], ['negsel'])
        ts('dve', Lcarry[:], rhzero[:, 0:32], iota_p[0:32, 0:1], 8.0, ALU.add, ALU.is_equal, ['rhzero', 'iota_p'], ['Lcarry'])
        mset('dve', onesb[:], 1.0, ['onesb'])
        mset('dve', one_c[:], 1.0, ['one_c'])
        mset('dve', eps_c[:], EPS, ['eps_c'])
        mset('dve', LAll[:], 0.0, ['LAll'])
        for i in range(8):
            if i > 0:
                mset('dve', LAll[:, i, 0:i], 1.0, ['LAll'])
            mset('dve', LAll[:, i, 8:9], 1.0, ['LAll'])

        dma(g_sb[:], norm_g[0].rearrange("(k p) -> p k", p=128), (), ['g_sb'], slow=True)
        for k in range(4):
            dma(cw_sb[:, :, k], conv_w[0, k].rearrange("(c p) -> p c", p=128), (), ['cw_sb'], slow=True)
        dma(cb_sb[:], conv_b[0].rearrange("(c p) -> p c", p=128), (), ['cb_sb'], slow=True)
        dma(hg_sb[:], hng[0].rearrange("(c p) -> p c", p=128), (), ['hg_sb'], slow=True)
        dma(gbias[:, 0:4], b_ig[0].partition_broadcast(128), (), ['gbias'])
        dma(gbias[:, 4:8], b_fg[0].partition_broadcast(128), (), ['gbias'])
        dma(fg_bc[:], fng.partition_broadcast(128), (), ['fg_bc'])
        ksc = 128.0 ** -0.5
        ts('dve', cw_sb[:, 4:8, :], cw_sb[:, 4:8, :], ksc, None, ALU.mult, None, ['cw_sb'], ['cw_sb'])
        ts('dve', cb_sb[:, 4:8], cb_sb[:, 4:8], ksc, None, ALU.mult, None, ['cb_sb'], ['cb_sb'])

        ci = 0
        cast_engs = ['dve', 'act', 'pool']
        for kc in range(8):
            for ch, (a, b_) in enumerate(WCH):
                buf, bk = xb[ci % 2], "xb%d" % (ci % 2)
                dma(buf[:, 0:b_ - a], w_in[0, kc * 128:(kc + 1) * 128, a:b_], (), [bk])
                copy_op(cast_engs[ci % 3], win[:, kc, a:b_], buf[:, 0:b_ - a], [bk, 'g_sb'], ["win.%d.%d" % (kc, ch)],
                        scale=g_sb[:, kc:kc + 1])
                ci += 1
        for kc in range(8):
            buf, bk = xb[ci % 2], "xb%d" % (ci % 2)
            dma(buf[:], w_out[0, kc * 128:(kc + 1) * 128, :], (), [bk])
            copy_op(cast_engs[ci % 3], wout[:, kc, :], buf[:], [bk, 'hg_sb'], ["wout.%d" % kc],
                    scale=(None if kc < 4 else hg_sb[:, kc - 4:kc - 3]))
            ci += 1
        xcnt = [ci]

        def next_xb():
            i = xcnt[0] % 2
            xcnt[0] += 1
            return xb[i], "xb%d" % i

        ev = [0]

        def evac_eng():
            ev[0] += 1
            return 'dve' if ev[0] % 3 else 'act'

        PJ = ['pj']

        def fm_proj(col0, ps_ap, pskeys):
            for kc in range(8):
                mm(ps_ap, win[:, kc, col0:col0 + 128], uT[:, kc, :], kc == 0, kc == 7,
                   wkeys(kc, col0, col0 + 128) + ['uT'], pskeys)

        def tm_proj(col0, ncols, j, ps_ap, pskeys):
            for kc in range(8):
                mm(ps_ap, uT[:, kc, j * 128:(j + 1) * 128], win[:, kc, col0:col0 + ncols], kc == 0, kc == 7,
                   wkeys(kc, col0, col0 + ncols) + ['uT'], pskeys)

        def rstd_ops(ss_ap, out_ap, inv_n, keys_r, key_w):
            act(out_ap, ss_ap, AF.Ln, list(keys_r) + ['eps_c'], [key_w], bias=eps_c[:, 0:1], scale=inv_n)
            act(out_ap, out_ap, AF.Exp, [key_w], [key_w], scale=-0.5)

        for b in range(NB if DBG_STOP > 0 else 0):
            mset('dve', Cst[:], 0.0, ['Cst%d' % h for h in range(4)])
            mset('dve', Cbf[:], 0.0, ['Cbf%d' % h for h in range(4)])
            mset('dve', hist[:], 0.0, ['hist%d' % c for c in range(8)])
            for T in range(NT):
                tok0 = T * TT
                for j in range(2):
                    xbuf, xk = next_xb()
                    r0 = tok0 + j * 128
                    dma(xbuf[:], x[b, r0:r0 + 128, :], (), [xk])
                    stt(ub[:], xbuf[:], 1.0, xbuf[:], ALU.mult, ALU.mult, [xk], ['ub', 'sm.ss%d' % j],
                        accum_out=sm[:, j:j + 1])
                    rstd_ops(sm[:, j:j + 1], sm[:, 2 + j:3 + j], 1.0 / D, ['sm.ss%d' % j], 'sm.rs%d' % j)
                    ts('dve', ub[:], xbuf[:], sm[:, 2 + j:3 + j], None, ALU.mult, None, [xk, 'sm.rs%d' % j], ['ub'])
                    for kc in range(8):
                        tr(tp[:, kc * 128:(kc + 1) * 128], ub[:, kc * 128:(kc + 1) * 128], ['ub'], ['tp'])
                    copy_op(evac_eng(), uT[:, :, j * 128:(j + 1) * 128], tp[:, :].rearrange("p (k t) -> p k t", k=8),
                            ['tp'], ['uT'])

                if DBG_STOP < 2:
                    continue
                for j in range(2):
                    tm_proj(C_SBV, 512, j, pj[:, :], PJ)
                    copy_op(evac_eng(), v_all[:, 2 * T + j, :], pj[:, :], PJ, ['v_all'])
                for hp in range(4):
                    q_t, qk_ = qT[hp % 2], 'qT%d' % (hp % 2)
                    sz_t, szk = szT[hp % 2], 'szT%d' % (hp % 2)
                    fm_proj(C_SBQ + hp * 128, pj[:, 0:256], ['pj'])
                    copy_op(evac_eng(), q_t[:], pj[:, 0:256], ['pj'], [qk_], scale=0.125)
                    fm_proj(C_SBK + hp * 128, pj[:, 256:512], ['pj'])
                    copy_op(evac_eng(), kT_all[:, hp, tok0:tok0 + TT], pj[:, 256:512], ['pj'], ['kT_all'])
                    fm_proj(C_SBZ + hp * 128, pj[:, 0:256], ['pj'])
                    act(sgf[:], pj[:, 0:256], AF.Exp, ['pj'], ['sgf'], scale=-1.0)
                    ts('dve', sgf[:], sgf[:], 1.0, None, ALU.add, None, ['sgf'], ['sgf'])
                    recip(sgf[:], sgf[:], ['sgf'], ['sgf'])
                    tt('dve', sz_t[:], pj[:, 0:256], sgf[:], ALU.mult, ['pj', 'sgf'], [szk])
                    for eh in range(2):
                        h = 2 * hp + eh
                        pl, ph = eh * 64, eh * 64 + 64
                        nkb = 2 * T + 2
                        segs = []
                        hi = nkb
                        while hi > 0:
                            lo = max(0, hi - 8)
                            segs.append((lo, hi))
                            hi = lo
                        first_av = True
                        for si, (lo, hi) in enumerate(segs):
                            npair = (hi - lo) // 2
                            if si == 0:
                                mm(rall[0:32, 0:256], Lcarry[:, :], rhzero[:, :], True, False, ['Lcarry', 'rhzero'], ['rall'])
                            else:
                                mm(rall[0:32, 0:256], Lcarry[:, :], RH[:, :], True, False, ['Lcarry', 'RH'], ['rall'])
                            for pi in range(npair):
                                kb0 = lo + 2 * pi
                                zt, zk = zb[pi % 2], 'zb%d' % (pi % 2)
                                e_t, ek = Et[pi % 2], 'E%d' % (pi % 2)
                                spk = 'SP%d' % pi
                                diag = (kb0 == 2 * T)
                                for hh in range(2):
                                    kb = kb0 + hh
                                    c0 = 128 if (diag and hh == 1) else 0
                                    mm(zt[:, hh * 256 + c0:hh * 256 + 256], kT_all[pl:ph, hp, kb * 128:(kb + 1) * 128],
                                       q_t[pl:ph, c0:256], True, True, ['kT_all', qk_], [zk])
                                rngs = [(0, 256), (384, 512)] if diag else [(0, 512)]
                                for (a, bb) in rngs:
                                    act(e_t[:, a:bb], zt[:, a:bb], AF.Exp, [zk], [ek])
                                for (a, bb) in rngs:
                                    act(SPst[:, pi, a:bb], e_t[:, a:bb], AF.Ln, [ek, 'one_c'], [spk], bias=one_c[:, 0:1])
                                if diag:
                                    for a in (0, 384):
                                        tt('pool', SPst[:, pi, a:a + 128], SPst[:, pi, a:a + 128], mstrict[:], ALU.mult,
                                           [spk, 'mstrict'], [spk])
                                for hh in range(2):
                                    kb = kb0 + hh
                                    c0 = 128 if (diag and hh == 1) else 0
                                    li = kb - lo
                                    last = (pi == npair - 1 and hh == 1)
                                    mm(rall[0:32, c0:256], LAll[:, li, :], SPst[:, pi, hh * 256 + c0:hh * 256 + 256],
                                       False, last, ['LAll', spk], ['rall'])
                            tcopy('dve', RH[:, :], rall[0:32, 0:256], ['rall'], ['RH'])
                            for pi in range(npair):
                                kb0 = lo + 2 * pi
                                zt, zk = zb[pi % 2], 'zb%d' % (pi % 2)
                                a_t, ak = At[pi % 2], 'A%d' % (pi % 2)
                                spk = 'SP%d' % pi
                                diag = (kb0 == 2 * T)
                                for hh in range(2):
                                    kb = kb0 + hh
                                    c0 = 128 if (diag and hh == 1) else 0
                                    li = kb - lo
                                    oz = zt[:, hh * 256 + c0:hh * 256 + 256]
                                    mm(oz, kT_all[pl:ph, hp, kb * 128:(kb + 1) * 128], q_t[pl:ph, c0:256], True, False,
                                       ['kT_all', qk_], [zk])
                                    mm(oz, negsel[:, li, :], RH[:, c0:256], False, False, ['negsel', 'RH'], [zk])
                                    mm(oz, negtri[:, :], SPst[:, pi, hh * 256 + c0:hh * 256 + 256], False, True,
                                       ['negtri', spk], [zk])
                                rngs = [(0, 256), (384, 512)] if diag else [(0, 512)]
                                for (a, bb) in rngs:
                                    act(a_t[:, a:bb], zt[:, a:bb], AF.Exp, [zk], [ak])
                                if diag:
                                    for a in (0, 384):
                                        tt('pool', a_t[:, a:a + 128], a_t[:, a:a + 128], mstrict[:], ALU.mult,
                                           [ak, 'mstrict'], [ak])
                                for hh in range(2):
                                    kb = kb0 + hh
                                    c0 = 128 if (diag and hh == 1) else 0
                                    is_last = (si == len(segs) - 1 and pi == npair - 1 and hh == 1)
                                    assert not (first_av and c0 != 0)
                                    mm(outTp[pl:ph, c0:256], v_all[:, kb, h * 64:(h + 1) * 64],
                                       a_t[:, hh * 256 + c0:hh * 256 + 256], first_av, is_last, ['v_all', ak], ['outTp'])
                                    first_av = False
                    tt('dve', mixT[:, hp, :], outTp[:, 0:256], sz_t[:], ALU.mult, ['outTp', szk], ['mixT'])

                if DBG_STOP < 3:
                    continue
                for j in range(2):
                    g0 = j * 20
                    tm_proj(C_MI, 8, j, mB[:, 400:408], ['mB'])
                    tt('dve', gates[:], mB[:, 400:408], gbias[:], ALU.add, ['mB', 'gbias'], ['gates'])
                    act(gsm[:, g0:g0 + 4], gates[:, 4:8], AF.Exp, ['gates'], ['gsm.nl%d' % j], scale=-1.0)
                    act(gsm[:, g0:g0 + 4], gsm[:, g0:g0 + 4], AF.Ln, ['gsm.nl%d' % j, 'one_c'], ['gsm.nl%d' % j],
                        bias=one_c[:, 0:1])
                    tcopy('dve', nlh[:, 0:4], gsm[:, g0:g0 + 4], ['gsm.nl%d' % j], ['nlh'])
                    tt('dve', nlh[:, 4:8], gsm[:, g0:g0 + 4], nlh[:, 0:4], ALU.subtract, ['gsm.nl%d' % j, 'nlh'], ['nll'])
                    mm(mB[:, 408:412], mincl[:, :], nlh[:, 0:4], True, False, ['mincl', 'nlh'], ['mB'])
                    mm(mB[:, 408:412], mincl[:, :], nlh[:, 4:8], False, True, ['mincl', 'nll'], ['mB'])
                    mm(mB[:, 412:416], onesb[:, :], nlh[:, 0:4], True, False, ['onesb', 'nlh'], ['mB'])
                    mm(mB[:, 412:416], onesb[:, :], nlh[:, 4:8], False, True, ['onesb', 'nll'], ['mB'])
                    act(gsm[:, g0 + 4:g0 + 8], mB[:, 408:412], AF.Exp, ['mB'], ['gsm.rf%d' % j], scale=-1.0)
                    tt('dve', gsm[:, g0 + 8:g0 + 12], mB[:, 408:412], gates[:, 0:4], ALU.add, ['mB', 'gates'],
                       ['gsm.cw%d' % j])
                    act(gsm[:, g0 + 8:g0 + 12], gsm[:, g0 + 8:g0 + 12], AF.Exp, ['gsm.cw%d' % j], ['gsm.cw%d' % j])
                    act(gsm[:, g0 + 12:g0 + 16], mB[:, 412:416], AF.Exp, ['mB'], ['gsm.eg%d' % j], scale=-1.0)
                    tt('dve', gsm[:, g0 + 16:g0 + 20], gsm[:, g0 + 8:g0 + 12], gsm[:, g0 + 12:g0 + 16], ALU.mult,
                       ['gsm.cw%d' % j, 'gsm.eg%d' % j], ['gsm.cu%d' % j])

                for h in range(4):
                    for qi, (cbase, dst, dk_) in enumerate([(C_MQ, qTm, 'qTm'), (C_MK, kTm, 'kTm')]):
                        ch = qi * 4 + h
                        pr, prk = pre[qi], 'pre%d' % qi
                        pkey = 'pj' if qi == 0 else 'pj'
                        pslc = pj[:, 256 * qi:256 * qi + 256]
                        fm_proj(cbase + h * 128, pslc, [pkey])
                        tcopy('pool', pr[:, 0:3], hist[:, ch, :], ['hist%d' % ch], [prk])
                        copy_op('act' if qi == 0 else 'dve', pr[:, 3:3 + TT], pslc, [pkey], [prk])
                        tcopy('pool', hist[:, ch, :], pr[:, TT:TT + 3], [prk], ['hist%d' % ch])
                        ts('dve', cacc[:], pr[:, 3:3 + TT], cw_sb[:, ch, 3:4], cb_sb[:, ch:ch + 1], ALU.mult, ALU.add,
                           [prk, 'cw_sb', 'cb_sb'], ['cacc'])
                        for k in (2, 1):
                            stt(cacc[:], pr[:, k:k + TT], cw_sb[:, ch, k:k + 1], cacc[:], ALU.mult, ALU.add,
                                [prk, 'cw_sb', 'cacc'], ['cacc'])
                        stt(dst[:], pr[:, 0:TT], cw_sb[:, ch, 0:1], cacc[:], ALU.mult, ALU.add,
                            [prk, 'cw_sb', 'cacc'], [dk_])
                    for j in range(2):
                        tr(tp[:, j * 128:(j + 1) * 128], kTm[:, j * 128:(j + 1) * 128], ['kTm'], ['tp'])
                    copy_op('dve', ktok[:, :, :], tp[:, 0:256].rearrange("p (k t) -> p k t", k=2), ['tp'], ['ktok'])
                    for j in range(2):
                        g0 = j * 20
                        jc = slice(j * 128, (j + 1) * 128)
                        for gi_, cb_ in enumerate((C_MV, C_MO, C_MZ)):
                            tm_proj(cb_ + h * 128, 128, j, mA[:, gi_ * 128:(gi_ + 1) * 128], ['mA'])
                        ts('dve', Vw[:, 0:128], mA[:, 0:128], gsm[:, g0 + 8 + h:g0 + 9 + h], None, ALU.mult, None,
                           ['mA', 'gsm.cw%d' % j], ['Vw'])
                        tcopy('pool', Vw[:, 128:129], gsm[:, g0 + 8 + h:g0 + 9 + h], ['gsm.cw%d' % j], ['Vw'])
                        ts('dve', Vu[:, 0:128], mA[:, 0:128], gsm[:, g0 + 16 + h:g0 + 17 + h], None, ALU.mult, None,
                           ['mA', 'gsm.cu%d' % j], ['Vu'])
                        tcopy('pool', Vu[:, 128:129], gsm[:, g0 + 16 + h:g0 + 17 + h], ['gsm.cu%d' % j], ['Vu'])
                        act(sgt[:], mA[:, 128:384], AF.Exp, ['mA'], ['sgt'], scale=-1.0)
                        ts('dve', sgt[:], sgt[:], 1.0, None, ALU.add, None, ['sgt'], ['sgt'])
                        recip(sgt[:], sgt[:], ['sgt'], ['sgt'])
                        tt('dve', gz[:], mA[:, 256:384], sgt[:, 128:256], ALU.mult, ['mA', 'sgt'], ['gz'])
                        mm(mB[:, 0:128], kTm[:, jc], qTm[:, jc], True, True, ['kTm', 'qTm'], ['mB'])
                        tt('dve', STm[:], mB[:, 0:128], mincl[:], ALU.mult, ['mB', 'mincl'], ['STm'])
                        mm(mB[:, 128:257], STm[:, :], Vw[:, :], True, False, ['STm', 'Vw'], ['mB'])
                        mm(mB[:, 128:257], qTm[:, jc], Cbf[:, h, :], False, True, ['qTm', 'Cbf%d' % h], ['mB'])
                        mm(mB[:, 258:387], ktok[:, j, :], Vu[:, :], True, True, ['ktok', 'Vu'], ['mB'])
                        stt(Cst[:, h, :], Cst[:, h, :], gsm[:, g0 + 12 + h:g0 + 13 + h], mB[:, 258:387], ALU.mult, ALU.add,
                            ['Cst%d' % h, 'gsm.eg%d' % j, 'mB'], ['Cst%d' % h])
                        tcopy('pool', Cbf[:, h, :], Cst[:, h, :], ['Cst%d' % h], ['Cbf%d' % h])
                        ts('dve', bst[:, 0:1], mB[:, 256:257], gsm[:, g0 + 4 + h:g0 + 5 + h], 1.0, ALU.mult, ALU.max,
                           ['mB', 'gsm.rf%d' % j], ['bst.d'])
                        ts('dve', bst[:, 2:3], mB[:, 256:257], gsm[:, g0 + 4 + h:g0 + 5 + h], -1.0, ALU.mult, ALU.mult,
                           ['mB', 'gsm.rf%d' % j], ['bst.n'])
                        tt('dve', bst[:, 0:1], bst[:, 0:1], bst[:, 2:3], ALU.max, ['bst.d', 'bst.n'], ['bst.d'])
                        recip(bst[:, 0:1], bst[:, 0:1], ['bst.d'], ['bst.d'])
                        tt('dve', bst[:, 1:2], bst[:, 0:1], gsm[:, g0 + 4 + h:g0 + 5 + h], ALU.mult,
                           ['bst.d', 'gsm.rf%d' % j], ['bst.sc'])
                        stt(hmo[:], mB[:, 128:256], bst[:, 1:2], sgt[:, 0:128], ALU.mult, ALU.mult,
                            ['mB', 'bst.sc', 'sgt'], ['hmo'])
                        op('dve', lambda e: e.bn_stats(out=sm[:, 8:14], in_=hmo[:]), ['hmo'], ['sm.bn'])
                        op('dve', lambda e: e.bn_aggr(out=sm[:, 16:18], in_=sm[:, 8:14]), ['sm.bn'], ['sm.mv'])
                        rstd_ops(sm[:, 17:18], sm[:, 18:19], 1.0, ['sm.mv'], 'sm.lrs')
                        ts('dve', yb[:], hmo[:], sm[:, 16:17], sm[:, 18:19], ALU.subtract, ALU.mult,
                           ['hmo', 'sm.mv', 'sm.lrs'], ['yb'])
                        tt('dve', ml[:], yb[:], gz[:], ALU.mult, ['yb', 'gz'], ['ml'])
                        tr(tp[:, 512:640], ml[:, :], ['ml'], ['tp'])
                        copy_op('act', mixT[:, 4 + h, j * 128:(j + 1) * 128], tp[:, 512:640], ['tp'], ['mixT'])

                if DBG_STOP < 4:
                    continue
                for j in range(2):
                    xbuf, xk = next_xb()
                    r0 = tok0 + j * 128
                    dma(xbuf[:], x[b, r0:r0 + 128, :], (), [xk])
                    for n in range(2):
                        pks = PJ if n == 0 else ['mA']
                        pt = pj if n == 0 else mA
                        for kc in range(8):
                            mm(pt[:, :], mixT[:, kc, j * 128:(j + 1) * 128], wout[:, kc, n * 512:(n + 1) * 512],
                               kc == 0, kc == 7, ['mixT', 'wout.%d' % kc], pks)
                        tt('dve', xbuf[:, n * 512:(n + 1) * 512], pt[:, :], xbuf[:, n * 512:(n + 1) * 512], ALU.add,
                           list(pks) + [xk], [xk])
                    stt(ub[:], xbuf[:], 1.0, xbuf[:], ALU.mult, ALU.mult, [xk], ['ub', 'sm.ss2'], accum_out=sm[:, 24:25])
                    rstd_ops(sm[:, 24:25], sm[:, 25:26], 1.0 / D, ['sm.ss2'], 'sm.rs2')
                    stt(xbuf[:], xbuf[:], sm[:, 25:26], fg_bc[:], ALU.mult, ALU.mult, [xk, 'sm.rs2', 'fg_bc'], [xk])
                    dma(out[b, r0:r0 + 128, :], xbuf[:], [xk], ())

        P.emit(nc, st)
    return nc


_CACHE = {}


def kernel(x, norm_g, w_in, b_igate, b_fgate, conv_w, conv_b, head_norm_g, w_out, final_norm_g):
    x = np.ascontiguousarray(x, dtype=np.float32)
    B, S, _ = x.shape
    ncores = 8
    NB = B // ncores
    key = (S, NB)
    if key not in _CACHE:
        _CACHE[key] = build(S, NB)
    nc = _CACHE[key]
    common = dict(norm_g=np.ascontiguousarray(norm_g, np.float32), w_in=np.ascontiguousarray(w_in, np.float32),
                  b_igate=np.ascontiguousarray(b_igate, np.float32), b_fgate=np.ascontiguousarray(b_fgate, np.float32),
                  conv_w=np.ascontiguousarray(conv_w, np.float32), conv_b=np.ascontiguousarray(conv_b, np.float32),
                  head_norm_g=np.ascontiguousarray(head_norm_g, np.float32),
                  w_out=np.ascontiguousarray(w_out, np.float32),
                  final_norm_g=np.ascontiguousarray(final_norm_g, np.float32))
    in_maps = []
    for c in range(ncores):
        m = dict(common)
        m["x"] = np.ascontiguousarray(x[c * NB:(c + 1) * NB])
        in_maps.append(m)
    res = run_bass_kernel_spmd(nc, in_maps, core_ids=list(range(ncores)))
    return np.concatenate([np.asarray(r["out"], dtype=np.float32) for r in res.results], axis=0)
```

```python
import numpy as np
import concourse.bass as bass
import concourse.mybir as mybir
from concourse.bass_utils import run_bass_kernel_spmd
from contextlib import ExitStack

F32 = mybir.dt.float32
BF16 = mybir.dt.bfloat16
I32 = mybir.dt.int32
AF = mybir.ActivationFunctionType
ALU = mybir.AluOpType

D = 1024
NIN = 4616
TT = 256
EPS = 1e-6
C_SBQ, C_SBK, C_SBV, C_SBZ = 0, 512, 1024, 1536
C_MQ, C_MK, C_MV, C_MO, C_MZ, C_MI = 2048, 2560, 3072, 3584, 4096, 4608
WCH = [(0, 1024), (1024, 2048), (2048, 3072), (3072, 4096), (4096, NIN)]
ND_SEM = 12
DBG_STOP = 99


class Prog:
    def __init__(self):
        self.ops = []
        self.lastw = {}
        self.readers = {}
        self.ndma = 0
        self.slot_last = {}
        self.excl = set()

    def op(self, eng, fn, reads=(), writes=(), dma=False):
        idx = len(self.ops)
        deps = {}
        for r in reads:
            w = self.lastw.get(r)
            if w is not None:
                deps[w] = True
            if r in self.excl:
                for e2, ridx in self.readers.get(r, {}).items():
                    if e2 != eng and not isinstance(ridx, list):
                        deps.setdefault(ridx, False)
        for r in writes:
            w = self.lastw.get(r)
            if w is not None:
                deps.setdefault(w, False)
            for ridx in self.readers.get(r, {}).values():
                if isinstance(ridx, list):
                    for q in ridx:
                        deps.setdefault(q, False)
                else:
                    deps.setdefault(ridx, False)
        for r in reads:
            d = self.readers.setdefault(r, {})
            if dma:
                d.setdefault('dma', []).append(idx)
            else:
                d[eng] = idx
        for r in writes:
            self.lastw[r] = idx
            self.readers[r] = {}
        slot = None
        if dma:
            slot = self.ndma % ND_SEM
            self.ndma += 1
            prev = self.slot_last.get(slot)
            if prev is not None:
                deps.setdefault(prev, False)
            self.slot_last[slot] = idx
        self.ops.append(dict(eng=eng, fn=fn, deps=deps, dma=dma, signal=False, slot=slot))
        return idx

    def emit(self, nc, stack):
        ops = self.ops
        for o in ops:
            for d, raw in o['deps'].items():
                p = ops[d]
                if p['dma'] or o['dma']:
                    p['signal'] = True
                elif p['eng'] != o['eng'] or raw or o['eng'] != 'pe':
                    p['signal'] = True
        sems = {e: stack.enter_context(nc.semaphore("sem_" + e)) for e in ['pe', 'act', 'dve', 'pool']}
        dsems = [stack.enter_context(nc.semaphore("dsem%d" % i)) for i in range(ND_SEM)]
        cnt = {e: 0 for e in sems}
        dcnt = [0] * ND_SEM
        nd = 0
        for o in ops:
            if o['dma']:
                k = o['slot']
                dcnt[k] += 16
                o['sig'] = (k, dcnt[k])
                o['signal'] = True
            elif o['signal']:
                cnt[o['eng']] += 1
                o['sig'] = (o['eng'], cnt[o['eng']])
        per_eng = {e: [] for e in ['pe', 'act', 'dve', 'pool', 'sp']}
        for o in ops:
            per_eng[o['eng']].append(o)
        final_d = list(dcnt)

        def run(engname, e):
            seen = {}
            for o in per_eng[engname]:
                waits = {}
                for d, raw in o['deps'].items():
                    p = ops[d]
                    if not (p['dma'] or o['dma']):
                        if p['eng'] == o['eng'] and not raw and o['eng'] == 'pe':
                            continue
                    key, val = p['sig']
                    if val > waits.get(key, 0):
                        waits[key] = val
                for key, val in waits.items():
                    if seen.get(key, 0) >= val:
                        continue
                    seen[key] = val
                    s = dsems[key] if isinstance(key, int) else sems[key]
                    e.wait_ge(s, val)
                ins = o['fn'](e)
                if o['signal']:
                    if o['dma']:
                        ins.then_inc(dsems[o['sig'][0]], 16)
                    else:
                        ins.then_inc(sems[o['eng']], 1)
            if engname == 'sp':
                for k in range(ND_SEM):
                    if final_d[k] > 0:
                        e.wait_ge(dsems[k], final_d[k])

        with nc.Block() as block:
            @block.sync
            def _(e):
                run('sp', e)

            @block.tensor
            def _(e):
                run('pe', e)

            @block.vector
            def _(e):
                run('dve', e)

            @block.scalar
            def _(e):
                run('act', e)

            @block.gpsimd
            def _(e):
                run('pool', e)


def build(S, NB):
    nc = bass.Bass("TRN2", target_bir_lowering=False)
    NT = S // TT
    NKB = S // 128
    x = nc.dram_tensor("x", [NB, S, D], F32, kind="ExternalInput").ap()
    norm_g = nc.dram_tensor("norm_g", [1, D], F32, kind="ExternalInput").ap()
    w_in = nc.dram_tensor("w_in", [1, D, NIN], F32, kind="ExternalInput").ap()
    b_ig = nc.dram_tensor("b_igate", [1, 4], F32, kind="ExternalInput").ap()
    b_fg = nc.dram_tensor("b_fgate", [1, 4], F32, kind="ExternalInput").ap()
    conv_w = nc.dram_tensor("conv_w", [1, 4, 1024], F32, kind="ExternalInput").ap()
    conv_b = nc.dram_tensor("conv_b", [1, 1024], F32, kind="ExternalInput").ap()
    hng = nc.dram_tensor("head_norm_g", [1, 512], F32, kind="ExternalInput").ap()
    w_out = nc.dram_tensor("w_out", [1, 1024, 1024], F32, kind="ExternalInput").ap()
    fng = nc.dram_tensor("final_norm_g", [1024], F32, kind="ExternalInput").ap()
    out = nc.dram_tensor("out", [NB, S, D], F32, kind="ExternalOutput").ap()

    P = Prog()
    with ExitStack() as st:
        def SB(name, shape, dt):
            return st.enter_context(nc.sbuf_tensor(name, shape, dt))

        def PS(name, shape, dt=F32):
            return st.enter_context(nc.psum_tensor(name, shape, dt))

        win = SB("win", [128, 8, NIN], BF16)
        wout = SB("wout", [128, 8, 1024], BF16)
        kT_all = SB("kT_all", [128, 4, S], BF16)
        v_all = SB("v_all", [128, NKB, 512], BF16)
        ident = SB("ident", [128, 128], BF16)
        negtri = SB("negtri", [128, 128], BF16)
        mstrict = SB("mstrict", [128, 128], BF16)
        mincl = SB("mincl", [128, 128], BF16)
        onesb = SB("onesb", [128, 128], BF16)
        nlh = SB("nlh", [128, 8], BF16)
        LAll = SB("LAll", [128, 8, 32], BF16)
        Lcarry = SB("Lcarry", [32, 32], BF16)
        negsel = SB("negsel", [32, 8, 128], BF16)
        rhzero = SB("rhzero", [32, 256], BF16)
        iota_p = SB("iota_p", [128, 1], F32)
        iota_f = SB("iota_f", [128, 128], F32)
        iota_i = SB("iota_i", [128, 128], I32)
        g_sb = SB("g_sb", [128, 8], F32)
        cw_sb = SB("cw_sb", [128, 8, 4], F32)
        cb_sb = SB("cb_sb", [128, 8], F32)
        hg_sb = SB("hg_sb", [128, 4], F32)
        gbias = SB("gbias", [128, 8], F32)
        fg_bc = SB("fg_bc", [128, 1024], F32)
        one_c = SB("one_c", [128, 1], F32)
        eps_c = SB("eps_c", [128, 1], F32)
        xb = [SB("xb%d" % i, [128, 1024], F32) for i in range(2)]
        ub = SB("ub", [128, 1024], BF16)
        uT = SB("uT", [128, 8, TT], BF16)
        sm = SB("sm", [128, 64], F32)
        qT = [SB("qT%d" % i, [128, TT], BF16) for i in range(2)]
        szT = [SB("szT%d" % i, [128, TT], BF16) for i in range(2)]
        sgf = SB("sgf", [128, TT], F32)
        Et = [SB("E%d" % i, [128, 512], BF16) for i in range(2)]
        SPst = SB("SPst", [128, 4, 512], BF16)
        At = [SB("A%d" % i, [128, 512], BF16) for i in range(2)]
        RH = SB("RH", [32, 256], BF16)
        hist = SB("hist", [128, 8, 3], F32)
        pre = [SB("pre%d" % i, [128, TT + 3], F32) for i in range(2)]
        cacc = SB("cacc", [128, TT], F32)
        qTm = SB("qTm", [128, TT], BF16)
        kTm = SB("kTm", [128, TT], BF16)
        ktok = SB("ktok", [128, 2, 128], BF16)
        Vw = SB("Vw", [128, 129], BF16)
        Vu = SB("Vu", [128, 129], BF16)
        sgt = SB("sgt", [128, 256], F32)
        gz = SB("gz", [128, 128], F32)
        STm = SB("STm", [128, 128], BF16)
        Cst = SB("Cst", [128, 4, 129], F32)
        Cbf = SB("Cbf", [128, 4, 129], BF16)
        hmo = SB("hmo", [128, 128], F32)
        yb = SB("yb", [128, 128], F32)
        ml = SB("ml", [128, 128], BF16)
        gates = SB("gates", [128, 8], F32)
        gsm = SB("gsm", [128, 40], F32)
        bst = SB("bst", [128, 8], F32)
        mixT = SB("mixT", [128, 8, TT], BF16)

        zb = [PS("zb%d" % i, [128, 512]) for i in range(2)]
        rall = PS("rall", [128, 512])
        outTp = PS("outTp", [128, 512])
        pj = PS("pj", [128, 512])
        mA = PS("mA", [128, 512])
        mB = PS("mB", [128, 512])
        tp = PS("tp", [128, 1024], BF16)

        op = P.op
        P.excl = {'zb0', 'zb1', 'rall', 'outTp', 'pj', 'mA', 'mB', 'tp'}

        def wkeys(kc, c0, c1):
            return ["win.%d.%d" % (kc, i) for i, (a, b) in enumerate(WCH) if a < c1 and b > c0]

        def mm(out_, lhsT, rhs, start, stop, reads, writes):
            op('pe', lambda e: e.matmul(out_, lhsT=lhsT, rhs=rhs, start=start, stop=stop), reads, writes)

        def tr(out_, in_, reads, writes):
            op('pe', lambda e: e.transpose(out_, in_, ident[:]), list(reads) + ['ident'], writes)

        def act(out_, in_, func, reads, writes, bias=None, scale=None):
            kw = {}
            if bias is not None:
                kw['bias'] = bias
            if scale is not None:
                kw['scale'] = scale
            op('act', lambda e: e.activation(out=out_, in_=in_, func=func, **kw), reads, writes)

        def ts(eng, out_, in0, s1, s2, op0, op1, reads, writes):
            if op1 is None:
                op(eng, lambda e: e.tensor_scalar(out=out_, in0=in0, scalar1=s1, scalar2=None, op0=op0), reads, writes)
            else:
                op(eng, lambda e: e.tensor_scalar(out=out_, in0=in0, scalar1=s1, scalar2=s2, op0=op0, op1=op1), reads, writes)

        def tt(eng, out_, in0, in1, aop, reads, writes):
            op(eng, lambda e: e.tensor_tensor(out=out_, in0=in0, in1=in1, op=aop), reads, writes)

        def stt(out_, in0, scalar, in1, op0, op1, reads, writes, accum_out=None):
            if accum_out is None:
                op('dve', lambda e: e.scalar_tensor_tensor(out=out_, in0=in0, scalar=scalar, in1=in1, op0=op0, op1=op1),
                   reads, writes)
            else:
                op('dve', lambda e: e.scalar_tensor_tensor(out=out_, in0=in0, scalar=scalar, in1=in1, op0=op0, op1=op1,
                                                           accum_out=accum_out), reads, writes)

        def tcopy(eng, out_, in_, reads, writes):
            op(eng, lambda e: e.tensor_copy(out=out_, in_=in_), reads, writes)

        def recip(out_, in_, reads, writes):
            op('dve', lambda e: e.reciprocal(out=out_, in_=in_), reads, writes)

        def mset(eng, ap, val, writes):
            op(eng, lambda e: e.memset(ap, val), (), writes)

        def dma(out_, in_, reads, writes, slow=False):
            if slow:
                op('sp', lambda e: e.dma_start(out=out_, in_=in_, allow_slow_non_contiguous=True), reads, writes, dma=True)
            else:
                op('sp', lambda e: e.dma_start(out=out_, in_=in_), reads, writes, dma=True)

        def copy_op(eng, out_ap, in_ap, reads, writes, scale=None):
            if eng == 'act':
                act(out_ap, in_ap, AF.Copy, reads, writes, scale=scale)
            elif scale is None:
                tcopy(eng, out_ap, in_ap, reads, writes)
            else:
                ts(eng, out_ap, in_ap, scale, None, ALU.mult, None, reads, writes)

        op('pool', lambda e: e.iota(iota_i[:], pattern=[[1, 128]], base=0, channel_multiplier=0), (), ['iota_i'])
        tcopy('dve', iota_f[:], iota_i[:], ['iota_i'], ['iota_f'])
        op('pool', lambda e: e.iota(iota_i[:, 0:1], pattern=[[0, 1]], base=0, channel_multiplier=1), ['iota_f'], ['iota_i'])
        tcopy('dve', iota_p[:], iota_i[:, 0:1], ['iota_i'], ['iota_p'])
        cs = ['iota_f', 'iota_p']
        ts('dve', ident[:], iota_f[:], iota_p[:, 0:1], None, ALU.is_equal, None, cs, ['ident'])
        ts('dve', mstrict[:], iota_f[:], iota_p[:, 0:1], None, ALU.is_gt, None, cs, ['mstrict'])
        ts('dve', mincl[:], iota_f[:], iota_p[:, 0:1], None, ALU.is_ge, None, cs, ['mincl'])
        ts('dve', negtri[:], iota_f[:], iota_p[:, 0:1], -1.0, ALU.is_le, ALU.mult, cs, ['negtri'])
        mset('dve', rhzero[:], 0.0, ['rhzero'])
        for i in range(8):
            ts('dve', negsel[:, i, :], rhzero[:, 0:128], iota_p[0:32, 0:1], float(i), ALU.add, ALU.is_equal,
               ['rhzero', 'iota_p'], ['negsel'])
        ts('dve', negsel[:], negsel[:], -1.0, None, ALU.mult, None, ['negsel'], ['negsel'])
        ts('dve', Lcarry[:], rhzero[:, 0:32], iota_p[0:32, 0:1], 8.0, ALU.add, ALU.is_equal, ['rhzero', 'iota_p'], ['Lcarry'])
        mset('dve', onesb[:], 1.0, ['onesb'])
        mset('dve', one_c[:], 1.0, ['one_c'])
        mset('dve', eps_c[:], EPS, ['eps_c'])
        mset('dve', LAll[:], 0.0, ['LAll'])
        for i in range(8):
            if i > 0:
                mset('dve', LAll[:, i, 0:i], 1.0, ['LAll'])
            mset('dve', LAll[:, i, 8:9], 1.0, ['LAll'])

        dma(g_sb[:], norm_g[0].rearrange("(k p) -> p k", p=128), (), ['g_sb'], slow=True)
        for k in range(4):
            dma(cw_sb[:, :, k], conv_w[0, k].rearrange("(c p) -> p c", p=128), (), ['cw_sb'], slow=True)
        dma(cb_sb[:], conv_b[0].rearrange("(c p) -> p c", p=128), (), ['cb_sb'], slow=True)
        dma(hg_sb[:], hng[0].rearrange("(c p) -> p c", p=128), (), ['hg_sb'], slow=True)
        dma(gbias[:, 0:4], b_ig[0].partition_broadcast(128), (), ['gbias'])
        dma(gbias[:, 4:8], b_fg[0].partition_broadcast(128), (), ['gbias'])
        dma(fg_bc[:], fng.partition_broadcast(128), (), ['fg_bc'])
        ksc = 128.0 ** -0.5
        ts('dve', cw_sb[:, 4:8, :], cw_sb[:, 4:8, :], ksc, None, ALU.mult, None, ['cw_sb'], ['cw_sb'])
        ts('dve', cb_sb[:, 4:8], cb_sb[:, 4:8], ksc, None, ALU.mult, None, ['cb_sb'], ['cb_sb'])

        ci = 0
        cast_engs = ['dve', 'act', 'pool']
        for kc in range(8):
            for ch, (a, b_) in enumerate(WCH):
                buf, bk = xb[ci % 2], "xb%d" % (ci % 2)
                dma(buf[:, 0:b_ - a], w_in[0, kc * 128:(kc + 1) * 128, a:b_], (), [bk])
                copy_op(cast_engs[ci % 3], win[:, kc, a:b_], buf[:, 0:b_ - a], [bk, 'g_sb'], ["win.%d.%d" % (kc, ch)],
                        scale=g_sb[:, kc:kc + 1])
                ci += 1
        for kc in range(8):
            buf, bk = xb[ci % 2], "xb%d" % (ci % 2)
            dma(buf[:], w_out[0, kc * 128:(kc + 1) * 128, :], (), [bk])
            copy_op(cast_engs[ci % 3], wout[:, kc, :], buf[:], [bk, 'hg_sb'], ["wout.%d" % kc],
                    scale=(None if kc < 4 else hg_sb[:, kc - 4:kc - 3]))
            ci += 1
        xcnt = [ci]

        def next_xb():
            i = xcnt[0] % 2
            xcnt[0] += 1
            return xb[i], "xb%d" % i

        ev = [0]

        def evac_eng():
            ev[0] += 1
            return 'dve' if ev[0] % 3 else 'act'

        PJ = ['pj']

        def fm_proj(col0, ps_ap, pskeys):
            for kc in range(8):
                mm(ps_ap, win[:, kc, col0:col0 + 128], uT[:, kc, :], kc == 0, kc == 7,
                   wkeys(kc, col0, col0 + 128) + ['uT'], pskeys)

        def tm_proj(col0, ncols, j, ps_ap, pskeys):
            for kc in range(8):
                mm(ps_ap, uT[:, kc, j * 128:(j + 1) * 128], win[:, kc, col0:col0 + ncols], kc == 0, kc == 7,
                   wkeys(kc, col0, col0 + ncols) + ['uT'], pskeys)

        def rstd_ops(ss_ap, out_ap, inv_n, keys_r, key_w):
            act(out_ap, ss_ap, AF.Ln, list(keys_r) + ['eps_c'], [key_w], bias=eps_c[:, 0:1], scale=inv_n)
            act(out_ap, out_ap, AF.Exp, [key_w], [key_w], scale=-0.5)

        for b in range(NB if DBG_STOP > 0 else 0):
            mset('dve', Cst[:], 0.0, ['Cst%d' % h for h in range(4)])
            mset('dve', Cbf[:], 0.0, ['Cbf%d' % h for h in range(4)])
            mset('dve', hist[:], 0.0, ['hist%d' % c for c in range(8)])
            for T in range(NT):
                tok0 = T * TT
                for j in range(2):
                    xbuf, xk = next_xb()
                    r0 = tok0 + j * 128
                    dma(xbuf[:], x[b, r0:r0 + 128, :], (), [xk])
                    stt(ub[:], xbuf[:], 1.0, xbuf[:], ALU.mult, ALU.mult, [xk], ['ub', 'sm.ss%d' % j],
                        accum_out=sm[:, j:j + 1])
                    rstd_ops(sm[:, j:j + 1], sm[:, 2 + j:3 + j], 1.0 / D, ['sm.ss%d' % j], 'sm.rs%d' % j)
                    ts('dve', ub[:], xbuf[:], sm[:, 2 + j:3 + j], None, ALU.mult, None, [xk, 'sm.rs%d' % j], ['ub'])
                    for kc in range(8):
                        tr(tp[:, kc * 128:(kc + 1) * 128], ub[:, kc * 128:(kc + 1) * 128], ['ub'], ['tp'])
                    copy_op(evac_eng(), uT[:, :, j * 128:(j + 1) * 128], tp[:, :].rearrange("p (k t) -> p k t", k=8),
                            ['tp'], ['uT'])

                def gen_B():
                    for j in range(2):
                        tm_proj(C_SBV, 512, j, pj[:, :], PJ)
                        copy_op(evac_eng(), v_all[:, 2 * T + j, :], pj[:, :], PJ, ['v_all'])
                        yield
                    for hp in range(4):
                        q_t, qk_ = qT[hp % 2], 'qT%d' % (hp % 2)
                        sz_t, szk = szT[hp % 2], 'szT%d' % (hp % 2)
                        fm_proj(C_SBQ + hp * 128, pj[:, 0:256], ['pj'])
                        copy_op(evac_eng(), q_t[:], pj[:, 0:256], ['pj'], [qk_], scale=0.125)
                        yield
                        fm_proj(C_SBK + hp * 128, pj[:, 256:512], ['pj'])
                        copy_op(evac_eng(), kT_all[:, hp, tok0:tok0 + TT], pj[:, 256:512], ['pj'], ['kT_all'])
                        yield
                        fm_proj(C_SBZ + hp * 128, pj[:, 0:256], ['pj'])
                        act(sgf[:], pj[:, 0:256], AF.Exp, ['pj'], ['sgf'], scale=-1.0)
                        ts('dve', sgf[:], sgf[:], 1.0, None, ALU.add, None, ['sgf'], ['sgf'])
                        recip(sgf[:], sgf[:], ['sgf'], ['sgf'])
                        tt('dve', sz_t[:], pj[:, 0:256], sgf[:], ALU.mult, ['pj', 'sgf'], [szk])
                        yield
                        for eh in range(2):
                            h = 2 * hp + eh
                            pl, ph = eh * 64, eh * 64 + 64
                            nkb = 2 * T + 2
                            segs = []
                            hi = nkb
                            while hi > 0:
                                lo = max(0, hi - 8)
                                segs.append((lo, hi))
                                hi = lo
                            first_av = True
                            for si, (lo, hi) in enumerate(segs):
                                npair = (hi - lo) // 2
                                if si == 0:
                                    mm(rall[0:32, 0:256], Lcarry[:, :], rhzero[:, :], True, False, ['Lcarry', 'rhzero'], ['rall'])
                                else:
                                    mm(rall[0:32, 0:256], Lcarry[:, :], RH[:, :], True, False, ['Lcarry', 'RH'], ['rall'])
                                for pi in range(npair):
                                    yield
                                    kb0 = lo + 2 * pi
                                    zt, zk = zb[pi % 2], 'zb%d' % (pi % 2)
                                    e_t, ek = Et[pi % 2], 'E%d' % (pi % 2)
                                    spk = 'SP%d' % pi
                                    diag = (kb0 == 2 * T)
                                    for hh in range(2):
                                        kb = kb0 + hh
                                        c0 = 128 if (diag and hh == 1) else 0
                                        mm(zt[:, hh * 256 + c0:hh * 256 + 256], kT_all[pl:ph, hp, kb * 128:(kb + 1) * 128],
                                           q_t[pl:ph, c0:256], True, True, ['kT_all', qk_], [zk])
                                    rngs = [(0, 256), (384, 512)] if diag else [(0, 512)]
                                    for (a, bb) in rngs:
                                        act(e_t[:, a:bb], zt[:, a:bb], AF.Exp, [zk], [ek])
                                    for (a, bb) in rngs:
                                        act(SPst[:, pi, a:bb], e_t[:, a:bb], AF.Ln, [ek, 'one_c'], [spk], bias=one_c[:, 0:1])
                                    if diag:
                                        for a in (0, 384):
                                            tt('pool', SPst[:, pi, a:a + 128], SPst[:, pi, a:a + 128], mstrict[:], ALU.mult,
                                               [spk, 'mstrict'], [spk])
                                    for hh in range(2):
                                        kb = kb0 + hh
                                        c0 = 128 if (diag and hh == 1) else 0
                                        li = kb - lo
                                        last = (pi == npair - 1 and hh == 1)
                                        mm(rall[0:32, c0:256], LAll[:, li, :], SPst[:, pi, hh * 256 + c0:hh * 256 + 256],
                                           False, last, ['LAll', spk], ['rall'])
                                tcopy('dve', RH[:, :], rall[0:32, 0:256], ['rall'], ['RH'])
                                for pi in range(npair):
                                    yield
                                    kb0 = lo + 2 * pi
                                    zt, zk = zb[pi % 2], 'zb%d' % (pi % 2)
                                    a_t, ak = At[pi % 2], 'A%d' % (pi % 2)
                                    spk = 'SP%d' % pi
                                    diag = (kb0 == 2 * T)
                                    for hh in range(2):
                                        kb = kb0 + hh
                                        c0 = 128 if (diag and hh == 1) else 0
                                        li = kb - lo
                                        oz = zt[:, hh * 256 + c0:hh * 256 + 256]
                                        mm(oz, kT_all[pl:ph, hp, kb * 128:(kb + 1) * 128], q_t[pl:ph, c0:256], True, False,
                                           ['kT_all', qk_], [zk])
                                        mm(oz, negsel[:, li, :], RH[:, c0:256], False, False, ['negsel', 'RH'], [zk])
                                        mm(oz, negtri[:, :], SPst[:, pi, hh * 256 + c0:hh * 256 + 256], False, True,
                                           ['negtri', spk], [zk])
                                    rngs = [(0, 256), (384, 512)] if diag else [(0, 512)]
                                    for (a, bb) in rngs:
                                        act(a_t[:, a:bb], zt[:, a:bb], AF.Exp, [zk], [ak])
                                    if diag:
                                        for a in (0, 384):
                                            tt('pool', a_t[:, a:a + 128], a_t[:, a:a + 128], mstrict[:], ALU.mult,
                                               [ak, 'mstrict'], [ak])
                                    for hh in range(2):
                                        kb = kb0 + hh
                                        c0 = 128 if (diag and hh == 1) else 0
                                        is_last = (si == len(segs) - 1 and pi == npair - 1 and hh == 1)
                                        assert not (first_av and c0 != 0)
                                        mm(outTp[pl:ph, c0:256], v_all[:, kb, h * 64:(h + 1) * 64],
                                           a_t[:, hh * 256 + c0:hh * 256 + 256], first_av, is_last, ['v_all', ak], ['outTp'])
                                        first_av = False
                        tt('dve', mixT[:, hp, :], outTp[:, 0:256], sz_t[:], ALU.mult, ['outTp', szk], ['mixT'])

                def gen_C():
                    for j in range(2):
                        g0 = j * 20
                        tm_proj(C_MI, 8, j, mB[:, 400:408], ['mB'])
                        tt('dve', gates[:], mB[:, 400:408], gbias[:], ALU.add, ['mB', 'gbias'], ['gates'])
                        act(gsm[:, g0:g0 + 4], gates[:, 4:8], AF.Exp, ['gates'], ['gsm.nl%d' % j], scale=-1.0)
                        act(gsm[:, g0:g0 + 4], gsm[:, g0:g0 + 4], AF.Ln, ['gsm.nl%d' % j, 'one_c'], ['gsm.nl%d' % j],
                            bias=one_c[:, 0:1])
                        tcopy('dve', nlh[:, 0:4], gsm[:, g0:g0 + 4], ['gsm.nl%d' % j], ['nlh'])
                        tt('dve', nlh[:, 4:8], gsm[:, g0:g0 + 4], nlh[:, 0:4], ALU.subtract, ['gsm.nl%d' % j, 'nlh'], ['nll'])
                        mm(mB[:, 408:412], mincl[:, :], nlh[:, 0:4], True, False, ['mincl', 'nlh'], ['mB'])
                        mm(mB[:, 408:412], mincl[:, :], nlh[:, 4:8], False, True, ['mincl', 'nll'], ['mB'])
                        mm(mB[:, 412:416], onesb[:, :], nlh[:, 0:4], True, False, ['onesb', 'nlh'], ['mB'])
                        mm(mB[:, 412:416], onesb[:, :], nlh[:, 4:8], False, True, ['onesb', 'nll'], ['mB'])
                        act(gsm[:, g0 + 4:g0 + 8], mB[:, 408:412], AF.Exp, ['mB'], ['gsm.rf%d' % j], scale=-1.0)
                        tt('dve', gsm[:, g0 + 8:g0 + 12], mB[:, 408:412], gates[:, 0:4], ALU.add, ['mB', 'gates'],
                           ['gsm.cw%d' % j])
                        act(gsm[:, g0 + 8:g0 + 12], gsm[:, g0 + 8:g0 + 12], AF.Exp, ['gsm.cw%d' % j], ['gsm.cw%d' % j])
                        act(gsm[:, g0 + 12:g0 + 16], mB[:, 412:416], AF.Exp, ['mB'], ['gsm.eg%d' % j], scale=-1.0)
                        tt('dve', gsm[:, g0 + 16:g0 + 20], gsm[:, g0 + 8:g0 + 12], gsm[:, g0 + 12:g0 + 16], ALU.mult,
                           ['gsm.cw%d' % j, 'gsm.eg%d' % j], ['gsm.cu%d' % j])
                        yield

                    for h in range(4):
                        for qi, (cbase, dst, dk_) in enumerate([(C_MQ, qTm, 'qTm'), (C_MK, kTm, 'kTm')]):
                            ch = qi * 4 + h
                            pr, prk = pre[qi], 'pre%d' % qi
                            pkey = 'pj' if qi == 0 else 'pj'
                            pslc = pj[:, 256 * qi:256 * qi + 256]
                            fm_proj(cbase + h * 128, pslc, [pkey])
                            tcopy('pool', pr[:, 0:3], hist[:, ch, :], ['hist%d' % ch], [prk])
                            copy_op('act' if qi == 0 else 'dve', pr[:, 3:3 + TT], pslc, [pkey], [prk])
                            tcopy('pool', hist[:, ch, :], pr[:, TT:TT + 3], [prk], ['hist%d' % ch])
                            ts('dve', cacc[:], pr[:, 3:3 + TT], cw_sb[:, ch, 3:4], cb_sb[:, ch:ch + 1], ALU.mult, ALU.add,
                               [prk, 'cw_sb', 'cb_sb'], ['cacc'])
                            for k in (2, 1):
                                stt(cacc[:], pr[:, k:k + TT], cw_sb[:, ch, k:k + 1], cacc[:], ALU.mult, ALU.add,
                                    [prk, 'cw_sb', 'cacc'], ['cacc'])
                            stt(dst[:], pr[:, 0:TT], cw_sb[:, ch, 0:1], cacc[:], ALU.mult, ALU.add,
                                [prk, 'cw_sb', 'cacc'], [dk_])
                            yield
                        for j in range(2):
                            tr(tp[:, j * 128:(j + 1) * 128], kTm[:, j * 128:(j + 1) * 128], ['kTm'], ['tp'])
                        copy_op('dve', ktok[:, :, :], tp[:, 0:256].rearrange("p (k t) -> p k t", k=2), ['tp'], ['ktok'])
                        yield
                        for j in range(2):
                            g0 = j * 20
                            jc = slice(j * 128, (j + 1) * 128)
                            for gi_, cb_ in enumerate((C_MV, C_MO, C_MZ)):
                                tm_proj(cb_ + h * 128, 128, j, mA[:, gi_ * 128:(gi_ + 1) * 128], ['mA'])
                            ts('dve', Vw[:, 0:128], mA[:, 0:128], gsm[:, g0 + 8 + h:g0 + 9 + h], None, ALU.mult, None,
                               ['mA', 'gsm.cw%d' % j], ['Vw'])
                            tcopy('pool', Vw[:, 128:129], gsm[:, g0 + 8 + h:g0 + 9 + h], ['gsm.cw%d' % j], ['Vw'])
                            ts('dve', Vu[:, 0:128], mA[:, 0:128], gsm[:, g0 + 16 + h:g0 + 17 + h], None, ALU.mult, None,
                               ['mA', 'gsm.cu%d' % j], ['Vu'])
                            tcopy('pool', Vu[:, 128:129], gsm[:, g0 + 16 + h:g0 + 17 + h], ['gsm.cu%d' % j], ['Vu'])
                            yield
                            act(sgt[:], mA[:, 128:384], AF.Exp, ['mA'], ['sgt'], scale=-1.0)
                            ts('dve', sgt[:], sgt[:], 1.0, None, ALU.add, None, ['sgt'], ['sgt'])
                            recip(sgt[:], sgt[:], ['sgt'], ['sgt'])
                            tt('dve', gz[:], mA[:, 256:384], sgt[:, 128:256], ALU.mult, ['mA', 'sgt'], ['gz'])
                            yield
                            mm(mB[:, 0:128], kTm[:, jc], qTm[:, jc], True, True, ['kTm', 'qTm'], ['mB'])
                            tt('dve', STm[:], mB[:, 0:128], mincl[:], ALU.mult, ['mB', 'mincl'], ['STm'])
                            yield
                            mm(mB[:, 128:257], STm[:, :], Vw[:, :], True, False, ['STm', 'Vw'], ['mB'])
                            mm(mB[:, 128:257], qTm[:, jc], Cbf[:, h, :], False, True, ['qTm', 'Cbf%d' % h], ['mB'])
                            mm(mB[:, 258:387], ktok[:, j, :], Vu[:, :], True, True, ['ktok', 'Vu'], ['mB'])
                            stt(Cst[:, h, :], Cst[:, h, :], gsm[:, g0 + 12 + h:g0 + 13 + h], mB[:, 258:387], ALU.mult, ALU.add,
                                ['Cst%d' % h, 'gsm.eg%d' % j, 'mB'], ['Cst%d' % h])
                            tcopy('pool', Cbf[:, h, :], Cst[:, h, :], ['Cst%d' % h], ['Cbf%d' % h])
                            yield
                            ts('dve', bst[:, 0:1], mB[:, 256:257], gsm[:, g0 + 4 + h:g0 + 5 + h], 1.0, ALU.mult, ALU.max,
                               ['mB', 'gsm.rf%d' % j], ['bst.d'])
                            ts('dve', bst[:, 2:3], mB[:, 256:257], gsm[:, g0 + 4 + h:g0 + 5 + h], -1.0, ALU.mult, ALU.mult,
                               ['mB', 'gsm.rf%d' % j], ['bst.n'])
                            tt('dve', bst[:, 0:1], bst[:, 0:1], bst[:, 2:3], ALU.max, ['bst.d', 'bst.n'], ['bst.d'])
                            recip(bst[:, 0:1], bst[:, 0:1], ['bst.d'], ['bst.d'])
                            tt('dve', bst[:, 1:2], bst[:, 0:1], gsm[:, g0 + 4 + h:g0 + 5 + h], ALU.mult,
                               ['bst.d', 'gsm.rf%d' % j], ['bst.sc'])
                            stt(hmo[:], mB[:, 128:256], bst[:, 1:2], sgt[:, 0:128], ALU.mult, ALU.mult,
                                ['mB', 'bst.sc', 'sgt'], ['hmo'])
                            yield
                            op('dve', lambda e: e.bn_stats(out=sm[:, 8:14], in_=hmo[:]), ['hmo'], ['sm.bn'])
                            op('dve', lambda e: e.bn_aggr(out=sm[:, 16:18], in_=sm[:, 8:14]), ['sm.bn'], ['sm.mv'])
                            rstd_ops(sm[:, 17:18], sm[:, 18:19], 1.0, ['sm.mv'], 'sm.lrs')
                            ts('dve', yb[:], hmo[:], sm[:, 16:17], sm[:, 18:19], ALU.subtract, ALU.mult,
                               ['hmo', 'sm.mv', 'sm.lrs'], ['yb'])
                            yield
                            tt('dve', ml[:], yb[:], gz[:], ALU.mult, ['yb', 'gz'], ['ml'])
                            tr(tp[:, 512:640], ml[:, :], ['ml'], ['tp'])
                            copy_op('act', mixT[:, 4 + h, j * 128:(j + 1) * 128], tp[:, 512:640], ['tp'], ['mixT'])
                            yield

                gens = []
                if DBG_STOP >= 2:
                    gens.append(gen_B())
                if DBG_STOP >= 3:
                    gens.append(gen_C())
                while gens:
                    for g in list(gens):
                        try:
                            next(g)
                        except StopIteration:
                            gens.remove(g)
                if DBG_STOP < 4:
                    continue
                for j in range(2):
                    xbuf, xk = next_xb()
                    r0 = tok0 + j * 128
                    dma(xbuf[:], x[b, r0:r0 + 128, :], (), [xk])
                    for n in range(2):
                        pks = PJ if n == 0 else ['mA']
                        pt = pj if n == 0 else mA
                        for kc in range(8):
                            mm(pt[:, :], mixT[:, kc, j * 128:(j + 1) * 128], wout[:, kc, n * 512:(n + 1) * 512],
                               kc == 0, kc == 7, ['mixT', 'wout.%d' % kc], pks)
                        tt('dve', xbuf[:, n * 512:(n + 1) * 512], pt[:, :], xbuf[:, n * 512:(n + 1) * 512], ALU.add,
                           list(pks) + [xk], [xk])
                    stt(ub[:], xbuf[:], 1.0, xbuf[:], ALU.mult, ALU.mult, [xk], ['ub', 'sm.ss2'], accum_out=sm[:, 24:25])
                    rstd_ops(sm[:, 24:25], sm[:, 25:26], 1.0 / D, ['sm.ss2'], 'sm.rs2')
                    stt(xbuf[:], xbuf[:], sm[:, 25:26], fg_bc[:], ALU.mult, ALU.mult, [xk, 'sm.rs2', 'fg_bc'], [xk])
                    dma(out[b, r0:r0 + 128, :], xbuf[:], [xk], ())

        P.emit(nc, st)
    return nc


_CACHE = {}


def kernel(x, norm_g, w_in, b_igate, b_fgate, conv_w, conv_b, head_norm_g, w_out, final_norm_g):
    x = np.ascontiguousarray(x, dtype=np.float32)
    B, S, _ = x.shape
    ncores = 8
    NB = B // ncores
    key = (S, NB)
    if key not in _CACHE:
        _CACHE[key] = build(S, NB)
    nc = _CACHE[key]
    common = dict(norm_g=np.ascontiguousarray(norm_g, np.float32), w_in=np.ascontiguousarray(w_in, np.float32),
                  b_igate=np.ascontiguousarray(b_igate, np.float32), b_fgate=np.ascontiguousarray(b_fgate, np.float32),
                  conv_w=np.ascontiguousarray(conv_w, np.float32), conv_b=np.ascontiguousarray(conv_b, np.float32),
                  head_norm_g=np.ascontiguousarray(head_norm_g, np.float32),
                  w_out=np.ascontiguousarray(w_out, np.float32),
                  final_norm_g=np.ascontiguousarray(final_norm_g, np.float32))
    in_maps = []
    for c in range(ncores):
        m = dict(common)
        m["x"] = np.ascontiguousarray(x[c * NB:(c + 1) * NB])
        in_maps.append(m)
    res = run_bass_kernel_spmd(nc, in_maps, core_ids=list(range(ncores)))
    return np.concatenate([np.asarray(r["out"], dtype=np.float32) for r in res.results], axis=0)
```

```python
import numpy as np
import concourse.bass as bass
import concourse.mybir as mybir
from concourse.bass_utils import run_bass_kernel_spmd
from contextlib import ExitStack

F32 = mybir.dt.float32
BF16 = mybir.dt.bfloat16
I32 = mybir.dt.int32
AF = mybir.ActivationFunctionType
ALU = mybir.AluOpType

D = 1024
NIN = 4616
TT = 256
EPS = 1e-6
C_SBQ, C_SBK, C_SBV, C_SBZ = 0, 512, 1024, 1536
C_MQ, C_MK, C_MV, C_MO, C_MZ, C_MI = 2048, 2560, 3072, 3584, 4096, 4608
WCH = [(0, 1024), (1024, 2048), (2048, 3072), (3072, 4096), (4096, NIN)]
ND_SEM = 12
DBG_STOP = 99


class Prog:
    def __init__(self):
        self.ops = []
        self.lastw = {}
        self.readers = {}
        self.ndma = 0
        self.slot_last = {}
        self.excl = set()

    def op(self, eng, fn, reads=(), writes=(), dma=False):
        idx = len(self.ops)
        deps = {}
        for r in reads:
            w = self.lastw.get(r)
            if w is not None:
                deps[w] = True
            if r in self.excl:
                for e2, ridx in self.readers.get(r, {}).items():
                    if e2 != eng and not isinstance(ridx, list):
                        deps.setdefault(ridx, False)
        for r in writes:
            w = self.lastw.get(r)
            if w is not None:
                deps.setdefault(w, False)
            for ridx in self.readers.get(r, {}).values():
                if isinstance(ridx, list):
                    for q in ridx:
                        deps.setdefault(q, False)
                else:
                    deps.setdefault(ridx, False)
        for r in reads:
            d = self.readers.setdefault(r, {})
            if dma:
                d.setdefault('dma', []).append(idx)
            else:
                d[eng] = idx
        for r in writes:
            self.lastw[r] = idx
            self.readers[r] = {}
        slot = None
        if dma:
            slot = self.ndma % ND_SEM
            self.ndma += 1
            prev = self.slot_last.get(slot)
            if prev is not None:
                deps.setdefault(prev, False)
            self.slot_last[slot] = idx
        self.ops.append(dict(eng=eng, fn=fn, deps=deps, dma=dma, signal=False, slot=slot))
        return idx

    def emit(self, nc, stack):
        ops = self.ops
        for o in ops:
            for d, raw in o['deps'].items():
                p = ops[d]
                if p['dma'] or o['dma']:
                    p['signal'] = True
                elif p['eng'] != o['eng'] or raw or o['eng'] != 'pe':
                    p['signal'] = True
        sems = {e: stack.enter_context(nc.semaphore("sem_" + e)) for e in ['pe', 'act', 'dve', 'pool']}
        dsems = [stack.enter_context(nc.semaphore("dsem%d" % i)) for i in range(ND_SEM)]
        cnt = {e: 0 for e in sems}
        dcnt = [0] * ND_SEM
        nd = 0
        for o in ops:
            if o['dma']:
                k = o['slot']
                dcnt[k] += 16
                o['sig'] = (k, dcnt[k])
                o['signal'] = True
            elif o['signal']:
                cnt[o['eng']] += 1
                o['sig'] = (o['eng'], cnt[o['eng']])
        per_eng = {e: [] for e in ['pe', 'act', 'dve', 'pool', 'sp']}
        for o in ops:
            per_eng[o['eng']].append(o)
        final_d = list(dcnt)

        def run(engname, e):
            seen = {}
            for o in per_eng[engname]:
                waits = {}
                for d, raw in o['deps'].items():
                    p = ops[d]
                    if not (p['dma'] or o['dma']):
                        if p['eng'] == o['eng'] and not raw and o['eng'] == 'pe':
                            continue
                    key, val = p['sig']
                    if val > waits.get(key, 0):
                        waits[key] = val
                for key, val in waits.items():
                    if seen.get(key, 0) >= val:
                        continue
                    seen[key] = val
                    s = dsems[key] if isinstance(key, int) else sems[key]
                    e.wait_ge(s, val)
                ins = o['fn'](e)
                if o['signal']:
                    if o['dma']:
                        ins.then_inc(dsems[o['sig'][0]], 16)
                    else:
                        ins.then_inc(sems[o['eng']], 1)
            if engname == 'sp':
                for k in range(ND_SEM):
                    if final_d[k] > 0:
                        e.wait_ge(dsems[k], final_d[k])

        with nc.Block() as block:
            @block.sync
            def _(e):
                run('sp', e)

            @block.tensor
            def _(e):
                run('pe', e)

            @block.vector
            def _(e):
                run('dve', e)

            @block.scalar
            def _(e):
                run('act', e)

            @block.gpsimd
            def _(e):
                run('pool', e)


def build(S, NB):
    nc = bass.Bass("TRN2", target_bir_lowering=False)
    NT = S // TT
    NKB = S // 128
    x = nc.dram_tensor("x", [NB, S, D], F32, kind="ExternalInput").ap()
    norm_g = nc.dram_tensor("norm_g", [1, D], F32, kind="ExternalInput").ap()
    w_in = nc.dram_tensor("w_in", [1, D, NIN], F32, kind="ExternalInput").ap()
    b_ig = nc.dram_tensor("b_igate", [1, 4], F32, kind="ExternalInput").ap()
    b_fg = nc.dram_tensor("b_fgate", [1, 4], F32, kind="ExternalInput").ap()
    conv_w = nc.dram_tensor("conv_w", [1, 4, 1024], F32, kind="ExternalInput").ap()
    conv_b = nc.dram_tensor("conv_b", [1, 1024], F32, kind="ExternalInput").ap()
    hng = nc.dram_tensor("head_norm_g", [1, 512], F32, kind="ExternalInput").ap()
    w_out = nc.dram_tensor("w_out", [1, 1024, 1024], F32, kind="ExternalInput").ap()
    fng = nc.dram_tensor("final_norm_g", [1024], F32, kind="ExternalInput").ap()
    out = nc.dram_tensor("out", [NB, S, D], F32, kind="ExternalOutput").ap()

    P = Prog()
    with ExitStack() as st:
        def SB(name, shape, dt):
            return st.enter_context(nc.sbuf_tensor(name, shape, dt))

        def PS(name, shape, dt=F32):
            return st.enter_context(nc.psum_tensor(name, shape, dt))

        win = SB("win", [128, 8, NIN], BF16)
        wout = SB("wout", [128, 8, 1024], BF16)
        kT_all = SB("kT_all", [128, 4, S], BF16)
        v_all = SB("v_all", [128, NKB, 512], BF16)
        ident = SB("ident", [128, 128], BF16)
        negtri = SB("negtri", [128, 128], BF16)
        mstrict = SB("mstrict", [128, 128], BF16)
        mincl = SB("mincl", [128, 128], BF16)
        onesb = SB("onesb", [128, 128], BF16)
        nlh = SB("nlh", [128, 8], BF16)
        LAll = SB("LAll", [128, 8, 32], BF16)
        Lcarry = SB("Lcarry", [32, 32], BF16)
        negsel = SB("negsel", [32, 8, 128], BF16)
        rhzero = SB("rhzero", [32, 256], BF16)
        iota_p = SB("iota_p", [128, 1], F32)
        iota_f = SB("iota_f", [128, 128], F32)
        iota_i = SB("iota_i", [128, 128], I32)
        g_sb = SB("g_sb", [128, 8], F32)
        cw_sb = SB("cw_sb", [128, 8, 4], F32)
        cb_sb = SB("cb_sb", [128, 8], F32)
        hg_sb = SB("hg_sb", [128, 4], F32)
        gbias = SB("gbias", [128, 8], F32)
        fg_bc = SB("fg_bc", [128, 1024], F32)
        one_c = SB("one_c", [128, 1], F32)
        eps_c = SB("eps_c", [128, 1], F32)
        negone_c = SB("negone_c", [128, 1], F32)
        xb = [SB("xb%d" % i, [128, 1024], F32) for i in range(2)]
        ub = SB("ub", [128, 1024], BF16)
        uT = SB("uT", [128, 8, TT], BF16)
        sm = SB("sm", [128, 64], F32)
        qT = [SB("qT%d" % i, [128, TT], BF16) for i in range(2)]
        szT = [SB("szT%d" % i, [128, TT], BF16) for i in range(2)]
        sgf = SB("sgf", [128, TT], F32)
        Et = [SB("E%d" % i, [128, 512], BF16) for i in range(2)]
        SPst = SB("SPst", [128, 8, 512], BF16)
        At = [SB("A%d" % i, [128, 512], BF16) for i in range(2)]
        RH = [SB("RH%d" % i, [32, 256], BF16) for i in range(2)]
        hist = SB("hist", [128, 8, 3], F32)
        pre = [SB("pre%d" % i, [128, TT + 3], F32) for i in range(2)]
        cacc = SB("cacc", [128, TT], F32)
        qTm = SB("qTm", [128, TT], BF16)
        kTm = SB("kTm", [128, TT], BF16)
        ktok = SB("ktok", [128, 2, 128], BF16)
        Vw = SB("Vw", [128, 129], BF16)
        Vu = SB("Vu", [128, 129], BF16)
        sgt = SB("sgt", [128, 256], F32)
        gz = SB("gz", [128, 128], F32)
        STm = SB("STm", [128, 128], BF16)
        Cst = SB("Cst", [128, 4, 129], F32)
        Cbf = SB("Cbf", [128, 4, 129], BF16)
        hmo = SB("hmo", [128, 128], F32)
        yb = SB("yb", [128, 128], F32)
        ml = SB("ml", [128, 128], BF16)
        gates = SB("gates", [128, 8], F32)
        gsm = SB("gsm", [128, 40], F32)
        bst = SB("bst", [128, 8], F32)
        mixT = SB("mixT", [128, 8, TT], BF16)

        zb = [PS("zb%d" % i, [128, 512]) for i in range(3)]
        rall = PS("rall", [128, 512])
        outTp = PS("outTp", [128, 512])
        pj = PS("pj", [128, 512])
        mA = PS("mA", [128, 512])
        tp = PS("tp", [128, 1024], BF16)

        op = P.op
        P.excl = {'zb0', 'zb1', 'zb2', 'rall', 'outTp', 'pj', 'mA', 'tp'}

        def wkeys(kc, c0, c1):
            return ["win.%d.%d" % (kc, i) for i, (a, b) in enumerate(WCH) if a < c1 and b > c0]

        def mm(out_, lhsT, rhs, start, stop, reads, writes):
            op('pe', lambda e: e.matmul(out_, lhsT=lhsT, rhs=rhs, start=start, stop=stop), reads, writes)

        def tr(out_, in_, reads, writes):
            op('pe', lambda e: e.transpose(out_, in_, ident[:]), list(reads) + ['ident'], writes)

        def act(out_, in_, func, reads, writes, bias=None, scale=None):
            kw = {}
            if bias is not None:
                kw['bias'] = bias
            if scale is not None:
                kw['scale'] = scale
            op('act', lambda e: e.activation(out=out_, in_=in_, func=func, **kw), reads, writes)

        def ts(eng, out_, in0, s1, s2, op0, op1, reads, writes):
            if op1 is None:
                op(eng, lambda e: e.tensor_scalar(out=out_, in0=in0, scalar1=s1, scalar2=None, op0=op0), reads, writes)
            else:
                op(eng, lambda e: e.tensor_scalar(out=out_, in0=in0, scalar1=s1, scalar2=s2, op0=op0, op1=op1), reads, writes)

        def tt(eng, out_, in0, in1, aop, reads, writes):
            op(eng, lambda e: e.tensor_tensor(out=out_, in0=in0, in1=in1, op=aop), reads, writes)

        def stt(out_, in0, scalar, in1, op0, op1, reads, writes, accum_out=None):
            if accum_out is None:
                op('dve', lambda e: e.scalar_tensor_tensor(out=out_, in0=in0, scalar=scalar, in1=in1, op0=op0, op1=op1),
                   reads, writes)
            else:
                op('dve', lambda e: e.scalar_tensor_tensor(out=out_, in0=in0, scalar=scalar, in1=in1, op0=op0, op1=op1,
                                                           accum_out=accum_out), reads, writes)

        def tcopy(eng, out_, in_, reads, writes):
            op(eng, lambda e: e.tensor_copy(out=out_, in_=in_), reads, writes)

        def recip(out_, in_, reads, writes):
            op('dve', lambda e: e.reciprocal(out=out_, in_=in_), reads, writes)

        def mset(eng, ap, val, writes):
            op(eng, lambda e: e.memset(ap, val), (), writes)

        def dma(out_, in_, reads, writes, slow=False):
            if slow:
                op('sp', lambda e: e.dma_start(out=out_, in_=in_, allow_slow_non_contiguous=True), reads, writes, dma=True)
            else:
                op('sp', lambda e: e.dma_start(out=out_, in_=in_), reads, writes, dma=True)

        def copy_op(eng, out_ap, in_ap, reads, writes, scale=None):
            if eng == 'act':
                act(out_ap, in_ap, AF.Copy, reads, writes, scale=scale)
            elif scale is None:
                tcopy(eng, out_ap, in_ap, reads, writes)
            else:
                ts(eng, out_ap, in_ap, scale, None, ALU.mult, None, reads, writes)

        op('pool', lambda e: e.iota(iota_i[:], pattern=[[1, 128]], base=0, channel_multiplier=0), (), ['iota_i'])
        tcopy('dve', iota_f[:], iota_i[:], ['iota_i'], ['iota_f'])
        op('pool', lambda e: e.iota(iota_i[:, 0:1], pattern=[[0, 1]], base=0, channel_multiplier=1), ['iota_f'], ['iota_i'])
        tcopy('dve', iota_p[:], iota_i[:, 0:1], ['iota_i'], ['iota_p'])
        cs = ['iota_f', 'iota_p']
        ts('dve', ident[:], iota_f[:], iota_p[:, 0:1], None, ALU.is_equal, None, cs, ['ident'])
        ts('dve', mstrict[:], iota_f[:], iota_p[:, 0:1], None, ALU.is_gt, None, cs, ['mstrict'])
        ts('dve', mincl[:], iota_f[:], iota_p[:, 0:1], None, ALU.is_ge, None, cs, ['mincl'])
        ts('dve', negtri[:], iota_f[:], iota_p[:, 0:1], -1.0, ALU.is_le, ALU.mult, cs, ['negtri'])
        mset('dve', rhzero[:], 0.0, ['rhzero'])
        for i in range(8):
            ts('dve', negsel[:, i, :], rhzero[:, 0:128], iota_p[0:32, 0:1], float(i), ALU.add, ALU.is_equal,
               ['rhzero', 'iota_p'], ['negsel'])
        ts('dve', negsel[:], negsel[:], -1.0, None, ALU.mult, None, ['negsel'], ['negsel'])
        ts('dve', Lcarry[:], rhzero[:, 0:32], iota_p[0:32, 0:1], 8.0, ALU.add, ALU.is_equal, ['rhzero', 'iota_p'], ['Lcarry'])
        mset('dve', onesb[:], 1.0, ['onesb'])
        mset('dve', one_c[:], 1.0, ['one_c'])
        mset('dve', eps_c[:], EPS, ['eps_c'])
        mset('dve', negone_c[:], -1.0, ['negone_c'])
        mset('dve', LAll[:], 0.0, ['LAll'])
        for i in range(8):
            if i > 0:
                mset('dve', LAll[:, i, 0:i], 1.0, ['LAll'])
            mset('dve', LAll[:, i, 8:9], 1.0, ['LAll'])

        dma(g_sb[:], norm_g[0].rearrange("(k p) -> p k", p=128), (), ['g_sb'], slow=True)
        for k in range(4):
            dma(cw_sb[:, :, k], conv_w[0, k].rearrange("(c p) -> p c", p=128), (), ['cw_sb'], slow=True)
        dma(cb_sb[:], conv_b[0].rearrange("(c p) -> p c", p=128), (), ['cb_sb'], slow=True)
        dma(hg_sb[:], hng[0].rearrange("(c p) -> p c", p=128), (), ['hg_sb'], slow=True)
        dma(gbias[:, 0:4], b_ig[0].partition_broadcast(128), (), ['gbias'])
        dma(gbias[:, 4:8], b_fg[0].partition_broadcast(128), (), ['gbias'])
        dma(fg_bc[:], fng.partition_broadcast(128), (), ['fg_bc'])
        ksc = 128.0 ** -0.5
        ts('dve', cw_sb[:, 4:8, :], cw_sb[:, 4:8, :], ksc, None, ALU.mult, None, ['cw_sb'], ['cw_sb'])
        ts('dve', cb_sb[:, 4:8], cb_sb[:, 4:8], ksc, None, ALU.mult, None, ['cb_sb'], ['cb_sb'])

        ci = 0
        cast_engs = ['dve', 'act', 'pool']
        for kc in range(8):
            for ch, (a, b_) in enumerate(WCH):
                buf, bk = xb[ci % 2], "xb%d" % (ci % 2)
                dma(buf[:, 0:b_ - a], w_in[0, kc * 128:(kc + 1) * 128, a:b_], (), [bk])
                copy_op(cast_engs[ci % 3], win[:, kc, a:b_], buf[:, 0:b_ - a], [bk, 'g_sb'], ["win.%d.%d" % (kc, ch)],
                        scale=g_sb[:, kc:kc + 1])
                ci += 1
        for kc in range(8):
            buf, bk = xb[ci % 2], "xb%d" % (ci % 2)
            dma(buf[:], w_out[0, kc * 128:(kc + 1) * 128, :], (), [bk])
            copy_op(cast_engs[ci % 3], wout[:, kc, :], buf[:], [bk, 'hg_sb'], ["wout.%d" % kc],
                    scale=(None if kc < 4 else hg_sb[:, kc - 4:kc - 3]))
            ci += 1
        xcnt = [ci]

        def next_xb():
            i = xcnt[0] % 2
            xcnt[0] += 1
            return xb[i], "xb%d" % i

        zbc = [0]

        def next_zb():
            i = zbc[0] % 3
            zbc[0] += 1
            return i

        ev = [0]

        def evac_eng():
            ev[0] += 1
            return 'dve' if ev[0] % 3 else 'act'

        PJ = ['pj']

        def fm_proj(col0, ps_ap, pskeys):
            for kc in range(8):
                mm(ps_ap, win[:, kc, col0:col0 + 128], uT[:, kc, :], kc == 0, kc == 7,
                   wkeys(kc, col0, col0 + 128) + ['uT'], pskeys)

        def tm_proj(col0, ncols, j, ps_ap, pskeys):
            for kc in range(8):
                mm(ps_ap, uT[:, kc, j * 128:(j + 1) * 128], win[:, kc, col0:col0 + ncols], kc == 0, kc == 7,
                   wkeys(kc, col0, col0 + ncols) + ['uT'], pskeys)

        def rstd_ops(ss_ap, out_ap, inv_n, keys_r, key_w):
            act(out_ap, ss_ap, AF.Ln, list(keys_r) + ['eps_c'], [key_w], bias=eps_c[:, 0:1], scale=inv_n)
            act(out_ap, out_ap, AF.Exp, [key_w], [key_w], scale=-0.5)

        for b in range(NB if DBG_STOP > 0 else 0):
            mset('dve', Cst[:], 0.0, ['Cst%d' % h for h in range(4)])
            mset('dve', Cbf[:], 0.0, ['Cbf%d' % h for h in range(4)])
            mset('dve', hist[:], 0.0, ['hist%d' % c for c in range(8)])
            for T in range(NT):
                tok0 = T * TT
                for j in range(2):
                    xbuf, xk = next_xb()
                    r0 = tok0 + j * 128
                    dma(xbuf[:], x[b, r0:r0 + 128, :], (), [xk])
                    stt(ub[:], xbuf[:], 1.0, xbuf[:], ALU.mult, ALU.mult, [xk], ['ub', 'sm.ss%d' % j],
                        accum_out=sm[:, j:j + 1])
                    rstd_ops(sm[:, j:j + 1], sm[:, 2 + j:3 + j], 1.0 / D, ['sm.ss%d' % j], 'sm.rs%d' % j)
                    ts('dve', ub[:], xbuf[:], sm[:, 2 + j:3 + j], None, ALU.mult, None, [xk, 'sm.rs%d' % j], ['ub'])
                    for kc in range(8):
                        tr(tp[:, kc * 128:(kc + 1) * 128], ub[:, kc * 128:(kc + 1) * 128], ['ub'], ['tp'])
                    copy_op(evac_eng(), uT[:, :, j * 128:(j + 1) * 128], tp[:, :].rearrange("p (k t) -> p k t", k=8),
                            ['tp'], ['uT'])

                def gen_B():
                    for j in range(2):
                        tm_proj(C_SBV, 512, j, pj[:, :], PJ)
                        copy_op(evac_eng(), v_all[:, 2 * T + j, :], pj[:, :], PJ, ['v_all'])
                        yield
                    nkb = 2 * T + 2
                    segs = []
                    hi_ = nkb
                    while hi_ > 0:
                        lo_ = max(0, hi_ - 8)
                        segs.append((lo_, hi_))
                        hi_ = lo_
                    nseg = len(segs)

                    def proj(hp):
                        q_t, qk_ = qT[hp % 2], 'qT%d' % (hp % 2)
                        sz_t, szk = szT[hp % 2], 'szT%d' % (hp % 2)
                        fm_proj(C_SBQ + hp * 128, pj[:, 0:256], ['pj'])
                        copy_op(evac_eng(), q_t[:], pj[:, 0:256], ['pj'], [qk_], scale=0.125)
                        yield
                        fm_proj(C_SBK + hp * 128, pj[:, 256:512], ['pj'])
                        copy_op(evac_eng(), kT_all[:, hp, tok0:tok0 + TT], pj[:, 256:512], ['pj'], ['kT_all'])
                        yield
                        fm_proj(C_SBZ + hp * 128, pj[:, 0:256], ['pj'])
                        act(sgf[:], pj[:, 0:256], AF.Exp, ['pj'], ['sgf'], scale=-1.0)
                        ts('dve', sgf[:], sgf[:], 1.0, None, ALU.add, None, ['sgf'], ['sgf'])
                        recip(sgf[:], sgf[:], ['sgf'], ['sgf'])
                        tt('dve', sz_t[:], pj[:, 0:256], sgf[:], ALU.mult, ['pj', 'sgf'], [szk])
                        yield

                    def geom(kb0, hh):
                        diag = (kb0 == 2 * T)
                        c0 = 128 if (diag and hh == 1) else 0
                        return diag, c0

                    def S1(hp, eh, si, par):
                        q_t, qk_ = qT[hp % 2], 'qT%d' % (hp % 2)
                        pl, ph = eh * 64, eh * 64 + 64
                        lo, hi = segs[si]
                        npair = (hi - lo) // 2
                        rh_t, rhk = RH[par], 'RH%d' % par
                        if si == 0:
                            mm(rall[0:32, 0:256], Lcarry[:, :], rhzero[:, :], True, False, ['Lcarry', 'rhzero'], ['rall'])
                        else:
                            mm(rall[0:32, 0:256], Lcarry[:, :], RH[1 - par][:, :], True, False,
                               ['Lcarry', 'RH%d' % (1 - par)], ['rall'])
                        for pi in range(npair):
                            kb0 = lo + 2 * pi
                            zi = next_zb()
                            zt, zk = zb[zi], 'zb%d' % zi
                            e_t, ek = Et[pi % 2], 'E%d' % (pi % 2)
                            spk = 'SP%d.%d' % (par, pi)
                            sps = SPst[:, par * 4 + pi, :]
                            diag = (kb0 == 2 * T)
                            for hh in range(2):
                                kb = kb0 + hh
                                _, c0 = geom(kb0, hh)
                                mm(zt[:, hh * 256 + c0:hh * 256 + 256], kT_all[pl:ph, hp, kb * 128:(kb + 1) * 128],
                                   q_t[pl:ph, c0:256], True, True, ['kT_all', qk_], [zk])
                            rngs = [(0, 256), (384, 512)] if diag else [(0, 512)]
                            for (a, bb) in rngs:
                                act(e_t[:, a:bb], zt[:, a:bb], AF.Exp, [zk], [ek])
                            for (a, bb) in rngs:
                                act(sps[:, a:bb], e_t[:, a:bb], AF.Ln, [ek, 'one_c'], [spk], bias=one_c[:, 0:1])
                            if diag:
                                for a in (0, 384):
                                    tt('pool', sps[:, a:a + 128], sps[:, a:a + 128], mstrict[:], ALU.mult,
                                       [spk, 'mstrict'], [spk])
                            yield
                            for hh in range(2):
                                kb = kb0 + hh
                                _, c0 = geom(kb0, hh)
                                li = kb - lo
                                last = (pi == npair - 1 and hh == 1)
                                mm(rall[0:32, c0:256], LAll[:, li, :], sps[:, hh * 256 + c0:hh * 256 + 256],
                                   False, last, ['LAll', spk], ['rall'])
                            if pi == npair - 1:
                                tcopy('dve', rh_t[:, :], rall[0:32, 0:256], ['rall'], [rhk])
                            yield

                    def S2(hp, eh, si, par):
                        h = 2 * hp + eh
                        q_t, qk_ = qT[hp % 2], 'qT%d' % (hp % 2)
                        sz_t, szk = szT[hp % 2], 'szT%d' % (hp % 2)
                        pl, ph = eh * 64, eh * 64 + 64
                        lo, hi = segs[si]
                        npair = (hi - lo) // 2
                        rh_t, rhk = RH[par], 'RH%d' % par

                        def Pmm(pi):
                            kb0 = lo + 2 * pi
                            zi = next_zb()
                            zt, zk = zb[zi], 'zb%d' % zi
                            spk = 'SP%d.%d' % (par, pi)
                            sps = SPst[:, par * 4 + pi, :]
                            for hh in range(2):
                                kb = kb0 + hh
                                _, c0 = geom(kb0, hh)
                                li = kb - lo
                                oz = zt[:, hh * 256 + c0:hh * 256 + 256]
                                mm(oz, kT_all[pl:ph, hp, kb * 128:(kb + 1) * 128], q_t[pl:ph, c0:256], True, False,
                                   ['kT_all', qk_], [zk])
                                mm(oz, negsel[:, li, :], rh_t[:, c0:256], False, False, ['negsel', rhk], [zk])
                                mm(oz, negtri[:, :], sps[:, hh * 256 + c0:hh * 256 + 256], False, True,
                                   ['negtri', spk], [zk])
                            return zt, zk

                        cur = Pmm(0)
                        yield
                        for pi in range(npair):
                            kb0 = lo + 2 * pi
                            zt, zk = cur
                            if pi + 1 < npair:
                                cur = Pmm(pi + 1)
                            a_t, ak = At[pi % 2], 'A%d' % (pi % 2)
                            diag = (kb0 == 2 * T)
                            rngs = [(0, 256), (384, 512)] if diag else [(0, 512)]
                            for (a, bb) in rngs:
                                act(a_t[:, a:bb], zt[:, a:bb], AF.Exp, [zk], [ak])
                            if diag:
                                for a in (0, 384):
                                    tt('pool', a_t[:, a:a + 128], a_t[:, a:a + 128], mstrict[:], ALU.mult,
                                       [ak, 'mstrict'], [ak])
                            yield
                            for hh in range(2):
                                kb = kb0 + hh
                                _, c0 = geom(kb0, hh)
                                first = (si == 0 and pi == 0 and hh == 0)
                                is_last = (si == nseg - 1 and pi == npair - 1 and hh == 1)
                                assert not (first and c0 != 0)
                                mm(outTp[pl:ph, c0:256], v_all[:, kb, h * 64:(h + 1) * 64],
                                   a_t[:, hh * 256 + c0:hh * 256 + 256], first, is_last, ['v_all', ak], ['outTp'])
                            if eh == 1 and si == nseg - 1 and pi == npair - 1:
                                tt('dve', mixT[:, hp, :], outTp[:, 0:256], sz_t[:], ALU.mult, ['outTp', szk], ['mixT'])
                            yield

                    for _ in proj(0):
                        yield
                    jobs = [(hp, eh, si) for hp in range(4) for eh in range(2) for si in range(nseg)]
                    prev = None
                    for k, job in enumerate(jobs):
                        streams = []
                        if prev is not None:
                            streams.append(S2(*prev))
                        streams.append(S1(job[0], job[1], job[2], k % 2))
                        if k % (2 * nseg) == 1 and job[0] + 1 < 4:
                            streams.append(proj(job[0] + 1))
                        while streams:
                            for g in list(streams):
                                try:
                                    next(g)
                                except StopIteration:
                                    streams.remove(g)
                            yield
                        prev = (job[0], job[1], job[2], k % 2)
                    for _ in S2(*prev):
                        yield

                def gen_C():
                    for j in range(2):
                        g0 = j * 20
                        tm_proj(C_MI, 8, j, mA[:, 496:504], ['mA'])
                        tt('dve', gates[:], mA[:, 496:504], gbias[:], ALU.add, ['mA', 'gbias'], ['gates'])
                        act(gsm[:, g0:g0 + 4], gates[:, 4:8], AF.Exp, ['gates'], ['gsm.nl%d' % j], scale=-1.0)
                        act(gsm[:, g0:g0 + 4], gsm[:, g0:g0 + 4], AF.Ln, ['gsm.nl%d' % j, 'one_c'], ['gsm.nl%d' % j],
                            bias=one_c[:, 0:1])
                        tcopy('dve', nlh[:, 0:4], gsm[:, g0:g0 + 4], ['gsm.nl%d' % j], ['nlh'])
                        tt('dve', nlh[:, 4:8], gsm[:, g0:g0 + 4], nlh[:, 0:4], ALU.subtract, ['gsm.nl%d' % j, 'nlh'], ['nll'])
                        mm(mA[:, 504:508], mincl[:, :], nlh[:, 0:4], True, False, ['mincl', 'nlh'], ['mA'])
                        mm(mA[:, 504:508], mincl[:, :], nlh[:, 4:8], False, True, ['mincl', 'nll'], ['mA'])
                        mm(mA[:, 508:512], onesb[:, :], nlh[:, 0:4], True, False, ['onesb', 'nlh'], ['mA'])
                        mm(mA[:, 508:512], onesb[:, :], nlh[:, 4:8], False, True, ['onesb', 'nll'], ['mA'])
                        act(gsm[:, g0 + 4:g0 + 8], mA[:, 504:508], AF.Exp, ['mA'], ['gsm.rf%d' % j], scale=-1.0)
                        tt('dve', gsm[:, g0 + 8:g0 + 12], mA[:, 504:508], gates[:, 0:4], ALU.add, ['mA', 'gates'],
                           ['gsm.cw%d' % j])
                        act(gsm[:, g0 + 8:g0 + 12], gsm[:, g0 + 8:g0 + 12], AF.Exp, ['gsm.cw%d' % j], ['gsm.cw%d' % j])
                        act(gsm[:, g0 + 12:g0 + 16], mA[:, 508:512], AF.Exp, ['mA'], ['gsm.eg%d' % j], scale=-1.0)
                        tt('dve', gsm[:, g0 + 16:g0 + 20], gsm[:, g0 + 8:g0 + 12], gsm[:, g0 + 12:g0 + 16], ALU.mult,
                           ['gsm.cw%d' % j, 'gsm.eg%d' % j], ['gsm.cu%d' % j])
                        yield

                    for h in range(4):
                        for qi, (cbase, dst, dk_) in enumerate([(C_MQ, qTm, 'qTm'), (C_MK, kTm, 'kTm')]):
                            ch = qi * 4 + h
                            pr, prk = pre[qi], 'pre%d' % qi
                            pkey = 'pj' if qi == 0 else 'pj'
                            pslc = pj[:, 256 * qi:256 * qi + 256]
                            fm_proj(cbase + h * 128, pslc, [pkey])
                            tcopy('pool', pr[:, 0:3], hist[:, ch, :], ['hist%d' % ch], [prk])
                            copy_op('act' if qi == 0 else 'dve', pr[:, 3:3 + TT], pslc, [pkey], [prk])
                            tcopy('pool', hist[:, ch, :], pr[:, TT:TT + 3], [prk], ['hist%d' % ch])
                            ts('dve', cacc[:], pr[:, 3:3 + TT], cw_sb[:, ch, 3:4], cb_sb[:, ch:ch + 1], ALU.mult, ALU.add,
                               [prk, 'cw_sb', 'cb_sb'], ['cacc'])
                            for k in (2, 1):
                                stt(cacc[:], pr[:, k:k + TT], cw_sb[:, ch, k:k + 1], cacc[:], ALU.mult, ALU.add,
                                    [prk, 'cw_sb', 'cacc'], ['cacc'])
                            stt(dst[:], pr[:, 0:TT], cw_sb[:, ch, 0:1], cacc[:], ALU.mult, ALU.add,
                                [prk, 'cw_sb', 'cacc'], [dk_])
                            yield
                        for j in range(2):
                            tr(tp[:, j * 128:(j + 1) * 128], kTm[:, j * 128:(j + 1) * 128], ['kTm'], ['tp'])
                        copy_op('dve', ktok[:, :, :], tp[:, 0:256].rearrange("p (k t) -> p k t", k=2), ['tp'], ['ktok'])
                        yield
                        for j in range(2):
                            g0 = j * 20
                            jc = slice(j * 128, (j + 1) * 128)
                            for gi_, cb_ in enumerate((C_MV, C_MO, C_MZ)):
                                tm_proj(cb_ + h * 128, 128, j, mA[:, gi_ * 128:(gi_ + 1) * 128], ['mA'])
                            ts('dve', Vw[:, 0:128], mA[:, 0:128], gsm[:, g0 + 8 + h:g0 + 9 + h], None, ALU.mult, None,
                               ['mA', 'gsm.cw%d' % j], ['Vw'])
                            tcopy('pool', Vw[:, 128:129], gsm[:, g0 + 8 + h:g0 + 9 + h], ['gsm.cw%d' % j], ['Vw'])
                            ts('dve', Vu[:, 0:128], mA[:, 0:128], gsm[:, g0 + 16 + h:g0 + 17 + h], None, ALU.mult, None,
                               ['mA', 'gsm.cu%d' % j], ['Vu'])
                            tcopy('pool', Vu[:, 128:129], gsm[:, g0 + 16 + h:g0 + 17 + h], ['gsm.cu%d' % j], ['Vu'])
                            yield
                            act(sgt[:], mA[:, 128:384], AF.Exp, ['mA'], ['sgt'], scale=-1.0)
                            ts('dve', sgt[:], sgt[:], 1.0, None, ALU.add, None, ['sgt'], ['sgt'])
                            recip(sgt[:], sgt[:], ['sgt'], ['sgt'])
                            tt('dve', gz[:], mA[:, 256:384], sgt[:, 128:256], ALU.mult, ['mA', 'sgt'], ['gz'])
                            yield
                            mm(mA[:, 384:512], kTm[:, jc], qTm[:, jc], True, True, ['kTm', 'qTm'], ['mA'])
                            tt('dve', STm[:], mA[:, 384:512], mincl[:], ALU.mult, ['mA', 'mincl'], ['STm'])
                            yield
                            mm(mA[:, 0:129], STm[:, :], Vw[:, :], True, False, ['STm', 'Vw'], ['mA'])
                            mm(mA[:, 0:129], qTm[:, jc], Cbf[:, h, :], False, True, ['qTm', 'Cbf%d' % h], ['mA'])
                            mm(mA[:, 130:259], ktok[:, j, :], Vu[:, :], True, True, ['ktok', 'Vu'], ['mA'])
                            stt(Cst[:, h, :], Cst[:, h, :], gsm[:, g0 + 12 + h:g0 + 13 + h], mA[:, 130:259], ALU.mult, ALU.add,
                                ['Cst%d' % h, 'gsm.eg%d' % j, 'mA'], ['Cst%d' % h])
                            tcopy('pool', Cbf[:, h, :], Cst[:, h, :], ['Cst%d' % h], ['Cbf%d' % h])
                            yield
                            ts('dve', bst[:, 0:1], mA[:, 128:129], gsm[:, g0 + 4 + h:g0 + 5 + h], 1.0, ALU.mult, ALU.max,
                               ['mA', 'gsm.rf%d' % j], ['bst.d'])
                            ts('dve', bst[:, 2:3], mA[:, 128:129], gsm[:, g0 + 4 + h:g0 + 5 + h], -1.0, ALU.mult, ALU.mult,
                               ['mA', 'gsm.rf%d' % j], ['bst.n'])
                            tt('dve', bst[:, 0:1], bst[:, 0:1], bst[:, 2:3], ALU.max, ['bst.d', 'bst.n'], ['bst.d'])
                            recip(bst[:, 0:1], bst[:, 0:1], ['bst.d'], ['bst.d'])
                            tt('dve', bst[:, 1:2], bst[:, 0:1], gsm[:, g0 + 4 + h:g0 + 5 + h], ALU.mult,
                               ['bst.d', 'gsm.rf%d' % j], ['bst.sc'])
                            stt(hmo[:], mA[:, 0:128], bst[:, 1:2], sgt[:, 0:128], ALU.mult, ALU.mult,
                                ['mA', 'bst.sc', 'sgt'], ['hmo'])
                            yield
                            op('dve', lambda e: e.bn_stats(out=sm[:, 8:14], in_=hmo[:]), ['hmo'], ['sm.bn'])
                            op('dve', lambda e: e.bn_aggr(out=sm[:, 16:18], in_=sm[:, 8:14]), ['sm.bn'], ['sm.mv'])
                            rstd_ops(sm[:, 17:18], sm[:, 18:19], 1.0, ['sm.mv'], 'sm.lrs')
                            ts('dve', yb[:], hmo[:], sm[:, 16:17], sm[:, 18:19], ALU.subtract, ALU.mult,
                               ['hmo', 'sm.mv', 'sm.lrs'], ['yb'])
                            yield
                            tt('dve', ml[:], yb[:], gz[:], ALU.mult, ['yb', 'gz'], ['ml'])
                            tr(tp[:, 512:640], ml[:, :], ['ml'], ['tp'])
                            copy_op('act', mixT[:, 4 + h, j * 128:(j + 1) * 128], tp[:, 512:640], ['tp'], ['mixT'])
                            yield

                gens = []
                if DBG_STOP >= 2:
                    gens.append(gen_B())
                if DBG_STOP >= 3:
                    gens.append(gen_C())
                while gens:
                    for g in list(gens):
                        try:
                            next(g)
                        except StopIteration:
                            gens.remove(g)
                if DBG_STOP < 4:
                    continue
                for j in range(2):
                    xbuf, xk = next_xb()
                    r0 = tok0 + j * 128
                    dma(xbuf[:], x[b, r0:r0 + 128, :], (), [xk])
                    for n in range(2):
                        pks = PJ if n == 0 else ['mA']
                        pt = pj if n == 0 else mA
                        for kc in range(8):
                            mm(pt[:, :], mixT[:, kc, j * 128:(j + 1) * 128], wout[:, kc, n * 512:(n + 1) * 512],
                               kc == 0, kc == 7, ['mixT', 'wout.%d' % kc], pks)
                        tt('dve', xbuf[:, n * 512:(n + 1) * 512], pt[:, :], xbuf[:, n * 512:(n + 1) * 512], ALU.add,
                           list(pks) + [xk], [xk])
                    stt(ub[:], xbuf[:], 1.0, xbuf[:], ALU.mult, ALU.mult, [xk], ['ub', 'sm.ss2'], accum_out=sm[:, 24:25])
                    rstd_ops(sm[:, 24:25], sm[:, 25:26], 1.0 / D, ['sm.ss2'], 'sm.rs2')
                    stt(xbuf[:], xbuf[:], sm[:, 25:26], fg_bc[:], ALU.mult, ALU.mult, [xk, 'sm.rs2', 'fg_bc'], [xk])
                    dma(out[b, r0:r0 + 128, :], xbuf[:], [xk], ())

        P.emit(nc, st)
    return nc


_CACHE = {}


def kernel(x, norm_g, w_in, b_igate, b_fgate, conv_w, conv_b, head_norm_g, w_out, final_norm_g):
    x = np.ascontiguousarray(x, dtype=np.float32)
    B, S, _ = x.shape
    ncores = 8
    NB = B // ncores
    key = (S, NB)
    if key not in _CACHE:
        _CACHE[key] = build(S, NB)
    nc = _CACHE[key]
    common = dict(norm_g=np.ascontiguousarray(norm_g, np.float32), w_in=np.ascontiguousarray(w_in, np.float32),
                  b_igate=np.ascontiguousarray(b_igate, np.float32), b_fgate=np.ascontiguousarray(b_fgate, np.float32),
                  conv_w=np.ascontiguousarray(conv_w, np.float32), conv_b=np.ascontiguousarray(conv_b, np.float32),
                  head_norm_g=np.ascontiguousarray(head_norm_g, np.float32),
                  w_out=np.ascontiguousarray(w_out, np.float32),
                  final_norm_g=np.ascontiguousarray(final_norm_g, np.float32))
    in_maps = []
    for c in range(ncores):
        m = dict(common)
        m["x"] = np.ascontiguousarray(x[c * NB:(c + 1) * NB])
        in_maps.append(m)
    res = run_bass_kernel_spmd(nc, in_maps, core_ids=list(range(ncores)))
    return np.concatenate([np.asarray(r["out"], dtype=np.float32) for r in res.results], axis=0)
```

```python
import numpy as np
import concourse.bass as bass
import concourse.mybir as mybir
from concourse.bass_utils import run_bass_kernel_spmd
from contextlib import ExitStack

F32 = mybir.dt.float32
BF16 = mybir.dt.bfloat16
I32 = mybir.dt.int32
AF = mybir.ActivationFunctionType
ALU = mybir.AluOpType

D = 1024
NIN = 4616
TT = 256
EPS = 1e-6
C_SBQ, C_SBK, C_SBV, C_SBZ = 0, 512, 1024, 1536
C_MQ, C_MK, C_MV, C_MO, C_MZ, C_MI = 2048, 2560, 3072, 3584, 4096, 4608
WCH = [(0, 1024), (1024, 2048), (2048, 3072), (3072, 4096), (4096, NIN)]
ND_SEM = 12
DBG_STOP = 99


class Prog:
    def __init__(self):
        self.ops = []
        self.lastw = {}
        self.readers = {}
        self.ndma = 0
        self.slot_last = {}
        self.excl = set()

    def op(self, eng, fn, reads=(), writes=(), dma=False):
        idx = len(self.ops)
        deps = {}
        for r in reads:
            w = self.lastw.get(r)
            if w is not None:
                deps[w] = True
            if r in self.excl:
                for e2, ridx in self.readers.get(r, {}).items():
                    if e2 != eng and not isinstance(ridx, list):
                        deps.setdefault(ridx, False)
        for r in writes:
            w = self.lastw.get(r)
            if w is not None:
                deps.setdefault(w, False)
            for ridx in self.readers.get(r, {}).values():
                if isinstance(ridx, list):
                    for q in ridx:
                        deps.setdefault(q, False)
                else:
                    deps.setdefault(ridx, False)
        for r in reads:
            d = self.readers.setdefault(r, {})
            if dma:
                d.setdefault('dma', []).append(idx)
            else:
                d[eng] = idx
        for r in writes:
            self.lastw[r] = idx
            self.readers[r] = {}
        slot = None
        if dma:
            slot = self.ndma % ND_SEM
            self.ndma += 1
            prev = self.slot_last.get(slot)
            if prev is not None:
                deps.setdefault(prev, False)
            self.slot_last[slot] = idx
        self.ops.append(dict(eng=eng, fn=fn, deps=deps, dma=dma, signal=False, slot=slot))
        return idx

    def emit(self, nc, stack):
        ops = self.ops
        for o in ops:
            for d, raw in o['deps'].items():
                p = ops[d]
                if p['dma'] or o['dma']:
                    p['signal'] = True
                elif p['eng'] != o['eng'] or raw or o['eng'] != 'pe':
                    p['signal'] = True
        sems = {e: stack.enter_context(nc.semaphore("sem_" + e)) for e in ['pe', 'act', 'dve', 'pool']}
        dsems = [stack.enter_context(nc.semaphore("dsem%d" % i)) for i in range(ND_SEM)]
        cnt = {e: 0 for e in sems}
        dcnt = [0] * ND_SEM
        nd = 0
        for o in ops:
            if o['dma']:
                k = o['slot']
                dcnt[k] += 16
                o['sig'] = (k, dcnt[k])
                o['signal'] = True
            elif o['signal']:
                cnt[o['eng']] += 1
                o['sig'] = (o['eng'], cnt[o['eng']])
        per_eng = {e: [] for e in ['pe', 'act', 'dve', 'pool', 'sp']}
        for o in ops:
            per_eng[o['eng']].append(o)
        final_d = list(dcnt)

        def run(engname, e):
            seen = {}
            for o in per_eng[engname]:
                waits = {}
                for d, raw in o['deps'].items():
                    p = ops[d]
                    if not (p['dma'] or o['dma']):
                        if p['eng'] == o['eng'] and not raw and o['eng'] == 'pe':
                            continue
                    key, val = p['sig']
                    if val > waits.get(key, 0):
                        waits[key] = val
                for key, val in waits.items():
                    if seen.get(key, 0) >= val:
                        continue
                    seen[key] = val
                    s = dsems[key] if isinstance(key, int) else sems[key]
                    e.wait_ge(s, val)
                ins = o['fn'](e)
                if o['signal']:
                    if o['dma']:
                        ins.then_inc(dsems[o['sig'][0]], 16)
                    else:
                        ins.then_inc(sems[o['eng']], 1)
            if engname == 'sp':
                for k in range(ND_SEM):
                    if final_d[k] > 0:
                        e.wait_ge(dsems[k], final_d[k])

        with nc.Block() as block:
            @block.sync
            def _(e):
                run('sp', e)

            @block.tensor
            def _(e):
                run('pe', e)

            @block.vector
            def _(e):
                run('dve', e)

            @block.scalar
            def _(e):
                run('act', e)

            @block.gpsimd
            def _(e):
                run('pool', e)


def build(S, NB):
    nc = bass.Bass("TRN2", target_bir_lowering=False)
    NT = S // TT
    NKB = S // 128
    x = nc.dram_tensor("x", [NB, S, D], F32, kind="ExternalInput").ap()
    norm_g = nc.dram_tensor("norm_g", [1, D], F32, kind="ExternalInput").ap()
    w_in = nc.dram_tensor("w_in", [1, D, NIN], F32, kind="ExternalInput").ap()
    b_ig = nc.dram_tensor("b_igate", [1, 4], F32, kind="ExternalInput").ap()
    b_fg = nc.dram_tensor("b_fgate", [1, 4], F32, kind="ExternalInput").ap()
    conv_w = nc.dram_tensor("conv_w", [1, 4, 1024], F32, kind="ExternalInput").ap()
    conv_b = nc.dram_tensor("conv_b", [1, 1024], F32, kind="ExternalInput").ap()
    hng = nc.dram_tensor("head_norm_g", [1, 512], F32, kind="ExternalInput").ap()
    w_out = nc.dram_tensor("w_out", [1, 1024, 1024], F32, kind="ExternalInput").ap()
    fng = nc.dram_tensor("final_norm_g", [1024], F32, kind="ExternalInput").ap()
    out = nc.dram_tensor("out", [NB, S, D], F32, kind="ExternalOutput").ap()

    P = Prog()
    with ExitStack() as st:
        def SB(name, shape, dt):
            return st.enter_context(nc.sbuf_tensor(name, shape, dt))

        def PS(name, shape, dt=F32):
            return st.enter_context(nc.psum_tensor(name, shape, dt))

        win = SB("win", [128, 8, NIN], BF16)
        wout = SB("wout", [128, 8, 1024], BF16)
        kT_all = SB("kT_all", [128, 4, S], BF16)
        v_all = SB("v_all", [128, NKB, 512], BF16)
        ident = SB("ident", [128, 128], BF16)
        negtri = SB("negtri", [128, 128], BF16)
        mstrict = SB("mstrict", [128, 128], BF16)
        mincl = SB("mincl", [128, 128], BF16)
        onesb = SB("onesb", [128, 128], BF16)
        nlh = SB("nlh", [128, 8], BF16)
        LAll = SB("LAll", [128, 8, 32], BF16)
        Lcarry = SB("Lcarry", [32, 32], BF16)
        negsel = SB("negsel", [32, 8, 128], BF16)
        rhzero = SB("rhzero", [32, 256], BF16)
        iota_p = SB("iota_p", [128, 1], F32)
        iota_f = SB("iota_f", [128, 128], F32)
        iota_i = SB("iota_i", [128, 128], I32)
        g_sb = SB("g_sb", [128, 8], F32)
        cw_sb = SB("cw_sb", [128, 8, 4], F32)
        cb_sb = SB("cb_sb", [128, 8], F32)
        hg_sb = SB("hg_sb", [128, 4], F32)
        gbias = SB("gbias", [128, 8], F32)
        fg_bc = SB("fg_bc", [128, 1024], F32)
        one_c = SB("one_c", [128, 1], F32)
        eps_c = SB("eps_c", [128, 1], F32)
        negone_c = SB("negone_c", [128, 1], F32)
        xb = [SB("xb%d" % i, [128, 1024], F32) for i in range(2)]
        ub = SB("ub", [128, 1024], BF16)
        uT = SB("uT", [128, 8, TT], BF16)
        sm = SB("sm", [128, 64], F32)
        qT = [SB("qT%d" % i, [128, TT], BF16) for i in range(2)]
        szT = [SB("szT%d" % i, [128, TT], BF16) for i in range(2)]
        sgf = SB("sgf", [128, TT], F32)
        Et = [SB("E%d" % i, [128, 512], BF16) for i in range(2)]
        SPst = SB("SPst", [128, 8, 512], BF16)
        At = [SB("A%d" % i, [128, 512], BF16) for i in range(2)]
        RH = [SB("RH%d" % i, [32, 256], BF16) for i in range(2)]
        hist = SB("hist", [128, 8, 3], F32)
        pre = [SB("pre%d" % i, [128, TT + 3], F32) for i in range(2)]
        cacc = SB("cacc", [128, TT], F32)
        qTm = SB("qTm", [128, TT], BF16)
        kTm = SB("kTm", [128, TT], BF16)
        ktok = SB("ktok", [128, 2, 128], BF16)
        Vw = SB("Vw", [128, 129], BF16)
        Vu = SB("Vu", [128, 129], BF16)
        sgt = SB("sgt", [128, 256], F32)
        gz = SB("gz", [128, 128], F32)
        STm = SB("STm", [128, 128], BF16)
        Cst = SB("Cst", [128, 4, 129], F32)
        Cbf = SB("Cbf", [128, 4, 129], BF16)
        hmo = SB("hmo", [128, 128], F32)
        yb = SB("yb", [128, 128], F32)
        mlb = [SB("ml%d" % i, [128, 128], BF16) for i in range(2)]
        mlc = [0]
        mix_written = [False]
        gates = SB("gates", [128, 8], F32)
        gsm = SB("gsm", [128, 40], F32)
        bst = SB("bst", [128, 8], F32)
        mixT = SB("mixT", [128, 8, TT], BF16)

        zb = [PS("zb%d" % i, [128, 512]) for i in range(3)]
        rall = PS("rall", [128, 512])
        outTp = PS("outTp", [128, 512])
        pj = PS("pj", [128, 512])
        mA = PS("mA", [128, 512])
        tp = PS("tp", [128, 1024], BF16)

        op = P.op
        P.excl = {'zb0', 'zb1', 'zb2', 'rall', 'outTp', 'pj', 'mA', 'tp'}

        def wkeys(kc, c0, c1):
            return ["win.%d.%d" % (kc, i) for i, (a, b) in enumerate(WCH) if a < c1 and b > c0]

        def mm(out_, lhsT, rhs, start, stop, reads, writes):
            op('pe', lambda e: e.matmul(out_, lhsT=lhsT, rhs=rhs, start=start, stop=stop), reads, writes)

        def tr(out_, in_, reads, writes):
            op('pe', lambda e: e.transpose(out_, in_, ident[:]), list(reads) + ['ident'], writes)

        def act(out_, in_, func, reads, writes, bias=None, scale=None):
            kw = {}
            if bias is not None:
                kw['bias'] = bias
            if scale is not None:
                kw['scale'] = scale
            op('act', lambda e: e.activation(out=out_, in_=in_, func=func, **kw), reads, writes)

        def ts(eng, out_, in0, s1, s2, op0, op1, reads, writes):
            if op1 is None:
                op(eng, lambda e: e.tensor_scalar(out=out_, in0=in0, scalar1=s1, scalar2=None, op0=op0), reads, writes)
            else:
                op(eng, lambda e: e.tensor_scalar(out=out_, in0=in0, scalar1=s1, scalar2=s2, op0=op0, op1=op1), reads, writes)

        def tt(eng, out_, in0, in1, aop, reads, writes):
            op(eng, lambda e: e.tensor_tensor(out=out_, in0=in0, in1=in1, op=aop), reads, writes)

        def stt(out_, in0, scalar, in1, op0, op1, reads, writes, accum_out=None):
            if accum_out is None:
                op('dve', lambda e: e.scalar_tensor_tensor(out=out_, in0=in0, scalar=scalar, in1=in1, op0=op0, op1=op1),
                   reads, writes)
            else:
                op('dve', lambda e: e.scalar_tensor_tensor(out=out_, in0=in0, scalar=scalar, in1=in1, op0=op0, op1=op1,
                                                           accum_out=accum_out), reads, writes)

        def tcopy(eng, out_, in_, reads, writes):
            op(eng, lambda e: e.tensor_copy(out=out_, in_=in_), reads, writes)

        def recip(out_, in_, reads, writes):
            op('dve', lambda e: e.reciprocal(out=out_, in_=in_), reads, writes)

        def mset(eng, ap, val, writes):
            op(eng, lambda e: e.memset(ap, val), (), writes)

        def dma(out_, in_, reads, writes, slow=False):
            if slow:
                op('sp', lambda e: e.dma_start(out=out_, in_=in_, allow_slow_non_contiguous=True), reads, writes, dma=True)
            else:
                op('sp', lambda e: e.dma_start(out=out_, in_=in_), reads, writes, dma=True)

        def copy_op(eng, out_ap, in_ap, reads, writes, scale=None):
            if eng == 'act':
                act(out_ap, in_ap, AF.Copy, reads, writes, scale=scale)
            elif scale is None:
                tcopy(eng, out_ap, in_ap, reads, writes)
            else:
                ts(eng, out_ap, in_ap, scale, None, ALU.mult, None, reads, writes)

        op('pool', lambda e: e.iota(iota_i[:], pattern=[[1, 128]], base=0, channel_multiplier=0), (), ['iota_i'])
        tcopy('dve', iota_f[:], iota_i[:], ['iota_i'], ['iota_f'])
        op('pool', lambda e: e.iota(iota_i[:, 0:1], pattern=[[0, 1]], base=0, channel_multiplier=1), ['iota_f'], ['iota_i'])
        tcopy('dve', iota_p[:], iota_i[:, 0:1], ['iota_i'], ['iota_p'])
        cs = ['iota_f', 'iota_p']
        ts('dve', ident[:], iota_f[:], iota_p[:, 0:1], None, ALU.is_equal, None, cs, ['ident'])
        ts('dve', mstrict[:], iota_f[:], iota_p[:, 0:1], None, ALU.is_gt, None, cs, ['mstrict'])
        ts('dve', mincl[:], iota_f[:], iota_p[:, 0:1], None, ALU.is_ge, None, cs, ['mincl'])
        ts('dve', negtri[:], iota_f[:], iota_p[:, 0:1], -1.0, ALU.is_le, ALU.mult, cs, ['negtri'])
        mset('dve', rhzero[:], 0.0, ['rhzero'])
        for i in range(8):
            ts('dve', negsel[:, i, :], rhzero[:, 0:128], iota_p[0:32, 0:1], float(i), ALU.add, ALU.is_equal,
               ['rhzero', 'iota_p'], ['negsel'])
        ts('dve', negsel[:], negsel[:], -1.0, None, ALU.mult, None, ['negsel'], ['negsel'])
        ts('dve', Lcarry[:], rhzero[:, 0:32], iota_p[0:32, 0:1], 8.0, ALU.add, ALU.is_equal, ['rhzero', 'iota_p'], ['Lcarry'])
        mset('dve', onesb[:], 1.0, ['onesb'])
        mset('dve', one_c[:], 1.0, ['one_c'])
        mset('dve', eps_c[:], EPS, ['eps_c'])
        mset('dve', negone_c[:], -1.0, ['negone_c'])
        mset('dve', LAll[:], 0.0, ['LAll'])
        for i in range(8):
            if i > 0:
                mset('dve', LAll[:, i, 0:i], 1.0, ['LAll'])
            mset('dve', LAll[:, i, 8:9], 1.0, ['LAll'])

        dma(g_sb[:], norm_g[0].rearrange("(k p) -> p k", p=128), (), ['g_sb'], slow=True)
        for k in range(4):
            dma(cw_sb[:, :, k], conv_w[0, k].rearrange("(c p) -> p c", p=128), (), ['cw_sb'], slow=True)
        dma(cb_sb[:], conv_b[0].rearrange("(c p) -> p c", p=128), (), ['cb_sb'], slow=True)
        dma(hg_sb[:], hng[0].rearrange("(c p) -> p c", p=128), (), ['hg_sb'], slow=True)
        dma(gbias[:, 0:4], b_ig[0].partition_broadcast(128), (), ['gbias'])
        dma(gbias[:, 4:8], b_fg[0].partition_broadcast(128), (), ['gbias'])
        dma(fg_bc[:], fng.partition_broadcast(128), (), ['fg_bc'])
        ksc = 128.0 ** -0.5
        ts('dve', cw_sb[:, 4:8, :], cw_sb[:, 4:8, :], ksc, None, ALU.mult, None, ['cw_sb'], ['cw_sb'])
        ts('dve', cb_sb[:, 4:8], cb_sb[:, 4:8], ksc, None, ALU.mult, None, ['cb_sb'], ['cb_sb'])

        ci = 0
        cast_engs = ['dve', 'act', 'pool']
        for kc in range(8):
            for ch, (a, b_) in enumerate(WCH):
                buf, bk = xb[ci % 2], "xb%d" % (ci % 2)
                dma(buf[:, 0:b_ - a], w_in[0, kc * 128:(kc + 1) * 128, a:b_], (), [bk])
                copy_op(cast_engs[ci % 3], win[:, kc, a:b_], buf[:, 0:b_ - a], [bk, 'g_sb'], ["win.%d.%d" % (kc, ch)],
                        scale=g_sb[:, kc:kc + 1])
                ci += 1
        for kc in range(8):
            buf, bk = xb[ci % 2], "xb%d" % (ci % 2)
            dma(buf[:], w_out[0, kc * 128:(kc + 1) * 128, :], (), [bk])
            copy_op(cast_engs[ci % 3], wout[:, kc, :], buf[:], [bk, 'hg_sb'], ["wout.%d" % kc],
                    scale=(None if kc < 4 else hg_sb[:, kc - 4:kc - 3]))
            ci += 1
        xcnt = [ci]

        def next_xb():
            i = xcnt[0] % 2
            xcnt[0] += 1
            return xb[i], "xb%d" % i

        zbc = [0]

        def next_zb():
            i = zbc[0] % 3
            zbc[0] += 1
            return i

        ev = [0]

        def evac_eng():
            ev[0] += 1
            return 'dve' if ev[0] % 3 else 'act'

        PJ = ['pj']

        def fm_proj(col0, ps_ap, pskeys):
            for kc in range(8):
                mm(ps_ap, win[:, kc, col0:col0 + 128], uT[:, kc, :], kc == 0, kc == 7,
                   wkeys(kc, col0, col0 + 128) + ['uT'], pskeys)

        def tm_proj(col0, ncols, j, ps_ap, pskeys):
            for kc in range(8):
                mm(ps_ap, uT[:, kc, j * 128:(j + 1) * 128], win[:, kc, col0:col0 + ncols], kc == 0, kc == 7,
                   wkeys(kc, col0, col0 + ncols) + ['uT'], pskeys)

        def rstd_ops(ss_ap, out_ap, inv_n, keys_r, key_w):
            act(out_ap, ss_ap, AF.Ln, list(keys_r) + ['eps_c'], [key_w], bias=eps_c[:, 0:1], scale=inv_n)
            act(out_ap, out_ap, AF.Exp, [key_w], [key_w], scale=-0.5)

        pendingD = [None]

        def gen_D(b, tok0):
            for j in range(2):
                xbuf, xk = next_xb()
                r0 = tok0 + j * 128
                dma(xbuf[:], x[b, r0:r0 + 128, :], (), [xk])
                for n in range(2):
                    pks = PJ if n == 0 else ['mA']
                    pt = pj if n == 0 else mA
                    assert not mix_written[0]
                    for kc in range(8):
                        mm(pt[:, :], mixT[:, kc, j * 128:(j + 1) * 128], wout[:, kc, n * 512:(n + 1) * 512],
                           kc == 0, kc == 7, ['mixT', 'wout.%d' % kc], pks)
                    tt('dve', xbuf[:, n * 512:(n + 1) * 512], pt[:, :], xbuf[:, n * 512:(n + 1) * 512], ALU.add,
                       list(pks) + [xk], [xk])
                    yield
                stt(ub[:], xbuf[:], 1.0, xbuf[:], ALU.mult, ALU.mult, [xk], ['ub', 'sm.ss2'], accum_out=sm[:, 24:25])
                rstd_ops(sm[:, 24:25], sm[:, 25:26], 1.0 / D, ['sm.ss2'], 'sm.rs2')
                stt(xbuf[:], xbuf[:], sm[:, 25:26], fg_bc[:], ALU.mult, ALU.mult, [xk, 'sm.rs2', 'fg_bc'], [xk])
                dma(out[b, r0:r0 + 128, :], xbuf[:], [xk], ())
                yield

        for b in range(NB if DBG_STOP > 0 else 0):
            mset('dve', Cst[:], 0.0, ['Cst%d' % h for h in range(4)])
            mset('dve', Cbf[:], 0.0, ['Cbf%d' % h for h in range(4)])
            mset('dve', hist[:], 0.0, ['hist%d' % c for c in range(8)])
            for T in range(NT):
                tok0 = T * TT
                for j in range(2):
                    xbuf, xk = next_xb()
                    r0 = tok0 + j * 128
                    dma(xbuf[:], x[b, r0:r0 + 128, :], (), [xk])
                    stt(ub[:], xbuf[:], 1.0, xbuf[:], ALU.mult, ALU.mult, [xk], ['ub', 'sm.ss%d' % j],
                        accum_out=sm[:, j:j + 1])
                    rstd_ops(sm[:, j:j + 1], sm[:, 2 + j:3 + j], 1.0 / D, ['sm.ss%d' % j], 'sm.rs%d' % j)
                    ts('dve', ub[:], xbuf[:], sm[:, 2 + j:3 + j], None, ALU.mult, None, [xk, 'sm.rs%d' % j], ['ub'])
                    for kc in range(8):
                        tr(tp[:, kc * 128:(kc + 1) * 128], ub[:, kc * 128:(kc + 1) * 128], ['ub'], ['tp'])
                    copy_op(evac_eng(), uT[:, :, j * 128:(j + 1) * 128], tp[:, :].rearrange("p (k t) -> p k t", k=8),
                            ['tp'], ['uT'])

                def gen_B():
                    for j in range(2):
                        tm_proj(C_SBV, 512, j, pj[:, :], PJ)
                        copy_op(evac_eng(), v_all[:, 2 * T + j, :], pj[:, :], PJ, ['v_all'])
                        yield
                    nkb = 2 * T + 2
                    segs = []
                    hi_ = nkb
                    while hi_ > 0:
                        lo_ = max(0, hi_ - 8)
                        segs.append((lo_, hi_))
                        hi_ = lo_
                    nseg = len(segs)

                    def proj(hp):
                        q_t, qk_ = qT[hp % 2], 'qT%d' % (hp % 2)
                        sz_t, szk = szT[hp % 2], 'szT%d' % (hp % 2)
                        fm_proj(C_SBQ + hp * 128, pj[:, 0:256], ['pj'])
                        copy_op(evac_eng(), q_t[:], pj[:, 0:256], ['pj'], [qk_], scale=0.125)
                        yield
                        fm_proj(C_SBK + hp * 128, pj[:, 256:512], ['pj'])
                        copy_op(evac_eng(), kT_all[:, hp, tok0:tok0 + TT], pj[:, 256:512], ['pj'], ['kT_all'])
                        yield
                        fm_proj(C_SBZ + hp * 128, pj[:, 0:256], ['pj'])
                        act(sgf[:], pj[:, 0:256], AF.Exp, ['pj'], ['sgf'], scale=-1.0)
                        ts('dve', sgf[:], sgf[:], 1.0, None, ALU.add, None, ['sgf'], ['sgf'])
                        recip(sgf[:], sgf[:], ['sgf'], ['sgf'])
                        tt('dve', sz_t[:], pj[:, 0:256], sgf[:], ALU.mult, ['pj', 'sgf'], [szk])
                        yield

                    def geom(kb0, hh):
                        diag = (kb0 == 2 * T)
                        c0 = 128 if (diag and hh == 1) else 0
                        return diag, c0

                    def S1(hp, eh, si, par):
                        q_t, qk_ = qT[hp % 2], 'qT%d' % (hp % 2)
                        pl, ph = eh * 64, eh * 64 + 64
                        lo, hi = segs[si]
                        npair = (hi - lo) // 2
                        rh_t, rhk = RH[par], 'RH%d' % par
                        if si == 0:
                            mm(rall[0:32, 0:256], Lcarry[:, :], rhzero[:, :], True, False, ['Lcarry', 'rhzero'], ['rall'])
                        else:
                            mm(rall[0:32, 0:256], Lcarry[:, :], RH[1 - par][:, :], True, False,
                               ['Lcarry', 'RH%d' % (1 - par)], ['rall'])
                        for pi in range(npair):
                            kb0 = lo + 2 * pi
                            zi = next_zb()
                            zt, zk = zb[zi], 'zb%d' % zi
                            e_t, ek = Et[pi % 2], 'E%d' % (pi % 2)
                            spk = 'SP%d.%d' % (par, pi)
                            sps = SPst[:, par * 4 + pi, :]
                            diag = (kb0 == 2 * T)
                            for hh in range(2):
                                kb = kb0 + hh
                                _, c0 = geom(kb0, hh)
                                mm(zt[:, hh * 256 + c0:hh * 256 + 256], kT_all[pl:ph, hp, kb * 128:(kb + 1) * 128],
                                   q_t[pl:ph, c0:256], True, True, ['kT_all', qk_], [zk])
                            rngs = [(0, 256), (384, 512)] if diag else [(0, 512)]
                            for (a, bb) in rngs:
                                act(e_t[:, a:bb], zt[:, a:bb], AF.Exp, [zk], [ek])
                            for (a, bb) in rngs:
                                act(sps[:, a:bb], e_t[:, a:bb], AF.Ln, [ek, 'one_c'], [spk], bias=one_c[:, 0:1])
                            if diag:
                                for a in (0, 384):
                                    tt('dve', sps[:, a:a + 128], sps[:, a:a + 128], mstrict[:], ALU.mult,
                                       [spk, 'mstrict'], [spk])
                            yield
                            for hh in range(2):
                                kb = kb0 + hh
                                _, c0 = geom(kb0, hh)
                                li = kb - lo
                                last = (pi == npair - 1 and hh == 1)
                                mm(rall[0:32, c0:256], LAll[:, li, :], sps[:, hh * 256 + c0:hh * 256 + 256],
                                   False, last, ['LAll', spk], ['rall'])
                            if pi == npair - 1:
                                tcopy('dve', rh_t[:, :], rall[0:32, 0:256], ['rall'], [rhk])
                            yield

                    def S2(hp, eh, si, par):
                        h = 2 * hp + eh
                        q_t, qk_ = qT[hp % 2], 'qT%d' % (hp % 2)
                        sz_t, szk = szT[hp % 2], 'szT%d' % (hp % 2)
                        pl, ph = eh * 64, eh * 64 + 64
                        lo, hi = segs[si]
                        npair = (hi - lo) // 2
                        rh_t, rhk = RH[par], 'RH%d' % par

                        def Pmm(pi):
                            kb0 = lo + 2 * pi
                            zi = next_zb()
                            zt, zk = zb[zi], 'zb%d' % zi
                            spk = 'SP%d.%d' % (par, pi)
                            sps = SPst[:, par * 4 + pi, :]
                            for hh in range(2):
                                kb = kb0 + hh
                                _, c0 = geom(kb0, hh)
                                li = kb - lo
                                oz = zt[:, hh * 256 + c0:hh * 256 + 256]
                                mm(oz, kT_all[pl:ph, hp, kb * 128:(kb + 1) * 128], q_t[pl:ph, c0:256], True, False,
                                   ['kT_all', qk_], [zk])
                                mm(oz, negsel[:, li, :], rh_t[:, c0:256], False, False, ['negsel', rhk], [zk])
                                mm(oz, negtri[:, :], sps[:, hh * 256 + c0:hh * 256 + 256], False, True,
                                   ['negtri', spk], [zk])
                            return zt, zk

                        cur = Pmm(0)
                        yield
                        for pi in range(npair):
                            kb0 = lo + 2 * pi
                            zt, zk = cur
                            if pi + 1 < npair:
                                cur = Pmm(pi + 1)
                            a_t, ak = At[pi % 2], 'A%d' % (pi % 2)
                            diag = (kb0 == 2 * T)
                            rngs = [(0, 256), (384, 512)] if diag else [(0, 512)]
                            for (a, bb) in rngs:
                                act(a_t[:, a:bb], zt[:, a:bb], AF.Exp, [zk], [ak])
                            if diag:
                                for a in (0, 384):
                                    tt('dve', a_t[:, a:a + 128], a_t[:, a:a + 128], mstrict[:], ALU.mult,
                                       [ak, 'mstrict'], [ak])
                            yield
                            for hh in range(2):
                                kb = kb0 + hh
                                _, c0 = geom(kb0, hh)
                                first = (si == 0 and pi == 0 and hh == 0)
                                is_last = (si == nseg - 1 and pi == npair - 1 and hh == 1)
                                assert not (first and c0 != 0)
                                mm(outTp[pl:ph, c0:256], v_all[:, kb, h * 64:(h + 1) * 64],
                                   a_t[:, hh * 256 + c0:hh * 256 + 256], first, is_last, ['v_all', ak], ['outTp'])
                            if eh == 1 and si == nseg - 1 and pi == npair - 1:
                                mix_written[0] = True
                                tt('dve', mixT[:, hp, :], outTp[:, 0:256], sz_t[:], ALU.mult, ['outTp', szk], ['mixT'])
                            yield

                    for _ in proj(0):
                        yield
                    jobs = [(hp, eh, si) for hp in range(4) for eh in range(2) for si in range(nseg)]
                    prev = None
                    for k, job in enumerate(jobs):
                        streams = []
                        if prev is not None:
                            streams.append(S2(*prev))
                        streams.append(S1(job[0], job[1], job[2], k % 2))
                        if k % (2 * nseg) == 1 and job[0] + 1 < 4:
                            streams.append(proj(job[0] + 1))
                        while streams:
                            for g in list(streams):
                                try:
                                    next(g)
                                except StopIteration:
                                    streams.remove(g)
                            yield
                        prev = (job[0], job[1], job[2], k % 2)
                    for _ in S2(*prev):
                        yield

                pend = []

                def flush_pend():
                    while pend:
                        mli, h_, j_ = pend.pop(0)
                        tr(tp[:, 512:640], mlb[mli][:, :], ['ml%d' % mli], ['tp'])
                        mix_written[0] = True
                        copy_op('act', mixT[:, 4 + h_, j_ * 128:(j_ + 1) * 128], tp[:, 512:640], ['tp'], ['mixT'])

                def gen_C():
                    for j in range(2):
                        g0 = j * 20
                        tm_proj(C_MI, 8, j, mA[:, 496:504], ['mA'])
                        tt('dve', gates[:], mA[:, 496:504], gbias[:], ALU.add, ['mA', 'gbias'], ['gates'])
                        act(gsm[:, g0:g0 + 4], gates[:, 4:8], AF.Exp, ['gates'], ['gsm.nl%d' % j], scale=-1.0)
                        act(gsm[:, g0:g0 + 4], gsm[:, g0:g0 + 4], AF.Ln, ['gsm.nl%d' % j, 'one_c'], ['gsm.nl%d' % j],
                            bias=one_c[:, 0:1])
                        tcopy('dve', nlh[:, 0:4], gsm[:, g0:g0 + 4], ['gsm.nl%d' % j], ['nlh'])
                        tt('dve', nlh[:, 4:8], gsm[:, g0:g0 + 4], nlh[:, 0:4], ALU.subtract, ['gsm.nl%d' % j, 'nlh'], ['nll'])
                        mm(mA[:, 504:508], mincl[:, :], nlh[:, 0:4], True, False, ['mincl', 'nlh'], ['mA'])
                        mm(mA[:, 504:508], mincl[:, :], nlh[:, 4:8], False, True, ['mincl', 'nll'], ['mA'])
                        mm(mA[:, 508:512], onesb[:, :], nlh[:, 0:4], True, False, ['onesb', 'nlh'], ['mA'])
                        mm(mA[:, 508:512], onesb[:, :], nlh[:, 4:8], False, True, ['onesb', 'nll'], ['mA'])
                        act(gsm[:, g0 + 4:g0 + 8], mA[:, 504:508], AF.Exp, ['mA'], ['gsm.rf%d' % j], scale=-1.0)
                        tt('dve', gsm[:, g0 + 8:g0 + 12], mA[:, 504:508], gates[:, 0:4], ALU.add, ['mA', 'gates'],
                           ['gsm.cw%d' % j])
                        act(gsm[:, g0 + 8:g0 + 12], gsm[:, g0 + 8:g0 + 12], AF.Exp, ['gsm.cw%d' % j], ['gsm.cw%d' % j])
                        act(gsm[:, g0 + 12:g0 + 16], mA[:, 508:512], AF.Exp, ['mA'], ['gsm.eg%d' % j], scale=-1.0)
                        tt('dve', gsm[:, g0 + 16:g0 + 20], gsm[:, g0 + 8:g0 + 12], gsm[:, g0 + 12:g0 + 16], ALU.mult,
                           ['gsm.cw%d' % j, 'gsm.eg%d' % j], ['gsm.cu%d' % j])
                        yield

                    for h in range(4):
                        for qi, (cbase, dst, dk_) in enumerate([(C_MQ, qTm, 'qTm'), (C_MK, kTm, 'kTm')]):
                            ch = qi * 4 + h
                            pr, prk = pre[qi], 'pre%d' % qi
                            pkey = 'pj' if qi == 0 else 'pj'
                            pslc = pj[:, 256 * qi:256 * qi + 256]
                            fm_proj(cbase + h * 128, pslc, [pkey])
                            tcopy('dve', pr[:, 0:3], hist[:, ch, :], ['hist%d' % ch], [prk])
                            copy_op('act' if qi == 0 else 'dve', pr[:, 3:3 + TT], pslc, [pkey], [prk])
                            tcopy('dve', hist[:, ch, :], pr[:, TT:TT + 3], [prk], ['hist%d' % ch])
                            ts('dve', cacc[:], pr[:, 3:3 + TT], cw_sb[:, ch, 3:4], cb_sb[:, ch:ch + 1], ALU.mult, ALU.add,
                               [prk, 'cw_sb', 'cb_sb'], ['cacc'])
                            for k in (2, 1):
                                stt(cacc[:], pr[:, k:k + TT], cw_sb[:, ch, k:k + 1], cacc[:], ALU.mult, ALU.add,
                                    [prk, 'cw_sb', 'cacc'], ['cacc'])
                            stt(dst[:], pr[:, 0:TT], cw_sb[:, ch, 0:1], cacc[:], ALU.mult, ALU.add,
                                [prk, 'cw_sb', 'cacc'], [dk_])
                            yield
                        for j in range(2):
                            tr(tp[:, j * 128:(j + 1) * 128], kTm[:, j * 128:(j + 1) * 128], ['kTm'], ['tp'])
                        copy_op('dve', ktok[:, :, :], tp[:, 0:256].rearrange("p (k t) -> p k t", k=2), ['tp'], ['ktok'])
                        yield
                        for j in range(2):
                            g0 = j * 20
                            jc = slice(j * 128, (j + 1) * 128)
                            for gi_, cb_ in enumerate((C_MV, C_MO, C_MZ)):
                                tm_proj(cb_ + h * 128, 128, j, mA[:, gi_ * 128:(gi_ + 1) * 128], ['mA'])
                            ts('dve', Vw[:, 0:128], mA[:, 0:128], gsm[:, g0 + 8 + h:g0 + 9 + h], None, ALU.mult, None,
                               ['mA', 'gsm.cw%d' % j], ['Vw'])
                            tcopy('dve', Vw[:, 128:129], gsm[:, g0 + 8 + h:g0 + 9 + h], ['gsm.cw%d' % j], ['Vw'])
                            ts('dve', Vu[:, 0:128], mA[:, 0:128], gsm[:, g0 + 16 + h:g0 + 17 + h], None, ALU.mult, None,
                               ['mA', 'gsm.cu%d' % j], ['Vu'])
                            tcopy('dve', Vu[:, 128:129], gsm[:, g0 + 16 + h:g0 + 17 + h], ['gsm.cu%d' % j], ['Vu'])
                            yield
                            act(sgt[:], mA[:, 128:384], AF.Exp, ['mA'], ['sgt'], scale=-1.0)
                            ts('dve', sgt[:], sgt[:], 1.0, None, ALU.add, None, ['sgt'], ['sgt'])
                            recip(sgt[:], sgt[:], ['sgt'], ['sgt'])
                            tt('dve', gz[:], mA[:, 256:384], sgt[:, 128:256], ALU.mult, ['mA', 'sgt'], ['gz'])
                            yield
                            mm(mA[:, 384:512], kTm[:, jc], qTm[:, jc], True, True, ['kTm', 'qTm'], ['mA'])
                            tt('dve', STm[:], mA[:, 384:512], mincl[:], ALU.mult, ['mA', 'mincl'], ['STm'])
                            flush_pend()
                            yield
                            mm(mA[:, 0:129], STm[:, :], Vw[:, :], True, False, ['STm', 'Vw'], ['mA'])
                            mm(mA[:, 0:129], qTm[:, jc], Cbf[:, h, :], False, True, ['qTm', 'Cbf%d' % h], ['mA'])
                            mm(mA[:, 130:259], ktok[:, j, :], Vu[:, :], True, True, ['ktok', 'Vu'], ['mA'])
                            stt(Cst[:, h, :], Cst[:, h, :], gsm[:, g0 + 12 + h:g0 + 13 + h], mA[:, 130:259], ALU.mult, ALU.add,
                                ['Cst%d' % h, 'gsm.eg%d' % j, 'mA'], ['Cst%d' % h])
                            copy_op('act', Cbf[:, h, :], Cst[:, h, :], ['Cst%d' % h], ['Cbf%d' % h])
                            yield
                            ts('dve', bst[:, 0:1], mA[:, 128:129], gsm[:, g0 + 4 + h:g0 + 5 + h], 1.0, ALU.mult, ALU.max,
                               ['mA', 'gsm.rf%d' % j], ['bst.d'])
                            ts('dve', bst[:, 2:3], mA[:, 128:129], gsm[:, g0 + 4 + h:g0 + 5 + h], -1.0, ALU.mult, ALU.mult,
                               ['mA', 'gsm.rf%d' % j], ['bst.n'])
                            tt('dve', bst[:, 0:1], bst[:, 0:1], bst[:, 2:3], ALU.max, ['bst.d', 'bst.n'], ['bst.d'])
                            recip(bst[:, 0:1], bst[:, 0:1], ['bst.d'], ['bst.d'])
                            tt('dve', bst[:, 1:2], bst[:, 0:1], gsm[:, g0 + 4 + h:g0 + 5 + h], ALU.mult,
                               ['bst.d', 'gsm.rf%d' % j], ['bst.sc'])
                            stt(hmo[:], mA[:, 0:128], bst[:, 1:2], sgt[:, 0:128], ALU.mult, ALU.mult,
                                ['mA', 'bst.sc', 'sgt'], ['hmo'])
                            yield
                            op('dve', lambda e: e.bn_stats(out=sm[:, 8:14], in_=hmo[:]), ['hmo'], ['sm.bn'])
                            op('dve', lambda e: e.bn_aggr(out=sm[:, 16:18], in_=sm[:, 8:14]), ['sm.bn'], ['sm.mv'])
                            rstd_ops(sm[:, 17:18], sm[:, 18:19], 1.0, ['sm.mv'], 'sm.lrs')
                            ts('dve', yb[:], hmo[:], sm[:, 16:17], sm[:, 18:19], ALU.subtract, ALU.mult,
                               ['hmo', 'sm.mv', 'sm.lrs'], ['yb'])
                            yield
                            mli = mlc[0] % 2
                            mlc[0] += 1
                            tt('dve', mlb[mli][:], yb[:], gz[:], ALU.mult, ['yb', 'gz'], ['ml%d' % mli])
                            pend.append((mli, h, j))
                            yield

                def gen_Cw():
                    for _ in gen_C():
                        yield
                    flush_pend()

                gens = []
                if pendingD[0] is not None:
                    gens.append(pendingD[0])
                    pendingD[0] = None
                mix_written[0] = False
                if DBG_STOP >= 2:
                    gens.append(gen_B())
                if DBG_STOP >= 3:
                    gens.append(gen_Cw())
                while gens:
                    for g in list(gens):
                        try:
                            next(g)
                        except StopIteration:
                            gens.remove(g)
                if DBG_STOP < 4:
                    continue
                pendingD[0] = gen_D(b, tok0)
        if pendingD[0] is not None:
            mix_written[0] = False
            for _ in pendingD[0]:
                pass

        P.emit(nc, st)
    return nc


_CACHE = {}


def kernel(x, norm_g, w_in, b_igate, b_fgate, conv_w, conv_b, head_norm_g, w_out, final_norm_g):
    x = np.ascontiguousarray(x, dtype=np.float32)
    B, S, _ = x.shape
    ncores = 8
    NB = B // ncores
    key = (S, NB)
    if key not in _CACHE:
        _CACHE[key] = build(S, NB)
    nc = _CACHE[key]
    common = dict(norm_g=np.ascontiguousarray(norm_g, np.float32), w_in=np.ascontiguousarray(w_in, np.float32),
                  b_igate=np.ascontiguousarray(b_igate, np.float32), b_fgate=np.ascontiguousarray(b_fgate, np.float32),
                  conv_w=np.ascontiguousarray(conv_w, np.float32), conv_b=np.ascontiguousarray(conv_b, np.float32),
                  head_norm_g=np.ascontiguousarray(head_norm_g, np.float32),
                  w_out=np.ascontiguousarray(w_out, np.float32),
                  final_norm_g=np.ascontiguousarray(final_norm_g, np.float32))
    in_maps = []
    for c in range(ncores):
        m = dict(common)
        m["x"] = np.ascontiguousarray(x[c * NB:(c + 1) * NB])
        in_maps.append(m)
    res = run_bass_kernel_spmd(nc, in_maps, core_ids=list(range(ncores)))
    return np.concatenate([np.asarray(r["out"], dtype=np.float32) for r in res.results], axis=0)
```
